# Optimizing a Trainium2 kernel written in Bass

```python
import jax, jax.numpy as jnp
from jax import lax
import numpy as np

D_MODEL = 4096
BATCH = 2
SEQ = 8192
DEPTH = 1

HEAD_DIM = 128
ATTN_GROUPS = ((128, 1), (512, 4), (2048, 16))
N_ATTN_GROUPS = 3
ATTN_HEADS_PER_GROUP = 8
ATTN_WIDTH = N_ATTN_GROUPS * ATTN_HEADS_PER_GROUP * HEAD_DIM
ATTN_OUT_WIDTH = ATTN_HEADS_PER_GROUP * HEAD_DIM
ROPE_DIM = HEAD_DIM // 4
ROPE_THETA = 500000.0
MLSTM_WIDTH = D_MODEL // 2
MLSTM_HEADS = 8
MLSTM_HEAD_DIM = MLSTM_WIDTH // MLSTM_HEADS
MLSTM_CHUNK = 128
CONV_WIDTH = 5
D_FF = 11008
NORM_EPS = 1e-6
NEG_INF = -1e30
IN_SIZES = (ATTN_WIDTH, ATTN_WIDTH, ATTN_WIDTH, MLSTM_WIDTH, MLSTM_WIDTH, MLSTM_WIDTH, MLSTM_WIDTH, 4 * MLSTM_HEADS, D_MODEL, D_MODEL)
IN_WIDTH = sum(IN_SIZES)

kernel_name = "hybrid_dilated_attn_mlstm_macaron"


def rms_norm(x, g):
    xf = x.astype(jnp.float32)
    y = xf * lax.rsqrt(jnp.mean(xf * xf, axis=-1, keepdims=True) + NORM_EPS)
    return (y * g.astype(jnp.float32)).astype(x.dtype)


def swiglu(h, w_gate, w_up, w_down):
    return (jax.nn.silu(h @ w_gate) * (h @ w_up)) @ w_down


def split_cols(u, sizes):
    offs = np.cumsum(np.array(sizes))[:-1].tolist()
    return jnp.split(u, offs, axis=-1)


def partial_rotary(x, pos):
    half = ROPE_DIM // 2
    inv_freq = ROPE_THETA ** (-jnp.arange(half, dtype=jnp.float32) * 2.0 / ROPE_DIM)
    ang = pos.astype(jnp.float32)[:, None] * inv_freq[None, :]
    cos = jnp.cos(ang)[None, :, None, :]
    sin = jnp.sin(ang)[None, :, None, :]
    xf = x.astype(jnp.float32)
    x1 = xf[..., :half]
    x2 = xf[..., half:ROPE_DIM]
    out = jnp.concatenate([x1 * cos - x2 * sin, x2 * cos + x1 * sin, xf[..., ROPE_DIM:]], axis=-1)
    return out.astype(x.dtype)


def dilated_window_attention(q, k, v, window, dilation):
    T = q.shape[1]
    dh = q.shape[-1]
    reach = window // (2 * dilation)
    blk = reach
    span = dilation * blk
    Tp = -(-T // span) * span
    S = Tp // dilation
    nb = S // blk

    def to_blocks(a):
        a = jnp.pad(a, [(0, 0), (0, Tp - T)] + [(0, 0)] * (a.ndim - 2))
        a = a.reshape((a.shape[0], S, dilation) + a.shape[2:])
        a = jnp.moveaxis(a, 2, 1)
        return a.reshape((a.shape[0], dilation, nb, blk) + a.shape[3:])

    def with_halo(a):
        ap = jnp.pad(a, [(0, 0), (0, 0), (1, 1)] + [(0, 0)] * (a.ndim - 3))
        return jnp.concatenate([ap[:, :, :-2], ap[:, :, 1:-1], ap[:, :, 2:]], axis=3)

    def from_blocks(a):
        a = a.reshape((a.shape[0], dilation, S) + a.shape[4:])
        a = jnp.moveaxis(a, 1, 2)
        return a.reshape((a.shape[0], Tp) + a.shape[3:])[:, :T]

    qb = to_blocks(q.astype(jnp.float32))
    kh = with_halo(to_blocks(k.astype(jnp.float32)))
    vh = with_halo(to_blocks(v.astype(jnp.float32)))
    kvalid = with_halo(to_blocks(jnp.ones((1, T), dtype=bool)))
    s = jnp.einsum('brnqhd,brnkhd->brnhqk', qb, kh) * (dh ** -0.5)
    a_idx = jnp.arange(blk)[:, None]
    c_idx = jnp.arange(3 * blk)[None, :]
    band = jnp.abs(c_idx - blk - a_idx) <= reach
    mask = band & kvalid[:, :, :, None, None, :]
    s = jnp.where(mask, s, NEG_INF)
    m = jnp.max(s, axis=-1, keepdims=True)
    p = jnp.exp(s - m)
    den = jnp.sum(p, axis=-1, keepdims=True)
    o = jnp.einsum('brnhqk,brnkhd->brnqhd', p / den, vh)
    lse = jnp.swapaxes((m + jnp.log(den))[..., 0], 3, 4)
    return from_blocks(o), from_blocks(lse)


def dilated_attention_branch(q, k, v, pos):
    B, T, _ = q.shape
    shp = (B, T, N_ATTN_GROUPS * ATTN_HEADS_PER_GROUP, HEAD_DIM)
    q = partial_rotary(q.reshape(shp), pos)
    k = partial_rotary(k.reshape(shp), pos)
    v = v.reshape(shp)
    outs, lses = [], []
    for g, (window, dilation) in enumerate(ATTN_GROUPS):
        sl = slice(g * ATTN_HEADS_PER_GROUP, (g + 1) * ATTN_HEADS_PER_GROUP)
        o, lse = dilated_window_attention(q[:, :, sl], k[:, :, sl], v[:, :, sl], window, dilation)
        outs.append(o)
        lses.append(lse)
    alpha = jax.nn.softmax(jnp.stack(lses, axis=0), axis=0)
    y = jnp.einsum('gbth,gbthd->bthd', alpha, jnp.stack(outs, axis=0))
    return y.reshape(B, T, ATTN_OUT_WIDTH).astype(q.dtype)


def centred_depthwise_conv(x, w, b):
    C = x.shape[-1]
    K = w.shape[0]
    left = (K - 1) // 2
    y = lax.conv_general_dilated(x, w[:, None, :].astype(x.dtype), window_strides=(1,), padding=[(left, K - 1 - left)], dimension_numbers=('NWC', 'WIO', 'NWC'), feature_group_count=C)
    return y + b.astype(x.dtype)


def mlstm_chunkwise(q, k, v, log_i, log_f):
    B, H, T, dk = q.shape
    dv = v.shape[-1]
    L = MLSTM_CHUNK
    nc = T // L

    def chunks(a):
        return jnp.moveaxis(a.reshape((B, H, nc, L) + a.shape[3:]), 2, 0)

    causal = jnp.tril(jnp.ones((L, L), dtype=bool))

    def step(carry, inp):
        C, n, m = carry
        qb, kb, vb, ib, fb = inp
        b = jnp.cumsum(fb, axis=-1)
        dmat = jnp.where(causal, b[..., :, None] - b[..., None, :] + ib[..., None, :], NEG_INF)
        inter = b + m[..., None]
        m_t = jnp.maximum(inter, jnp.max(dmat, axis=-1))
        w_inter = jnp.exp(inter - m_t)
        w_intra = jnp.exp(dmat - m_t[..., None])
        qk = jnp.einsum('bhld,bhsd->bhls', qb, kb) * w_intra
        num = w_inter[..., None] * jnp.einsum('bhld,bhde->bhle', qb, C) + jnp.einsum('bhls,bhse->bhle', qk, vb)
        den = w_inter * jnp.einsum('bhld,bhd->bhl', qb, n) + jnp.sum(qk, axis=-1)
        h = num / jnp.maximum(jnp.abs(den), jnp.exp(-m_t))[..., None]
        b_last = b[..., -1]
        g = b_last[..., None] - b + ib
        m_new = jnp.maximum(b_last + m, jnp.max(g, axis=-1))
        decay = jnp.exp(b_last + m - m_new)
        wk = jnp.exp(g - m_new[..., None])
        C_new = decay[..., None, None] * C + jnp.einsum('bhs,bhsd,bhse->bhde', wk, kb, vb)
        n_new = decay[..., None] * n + jnp.einsum('bhs,bhsd->bhd', wk, kb)
        return (C_new, n_new, m_new), h

    init = (jnp.zeros((B, H, dk, dv), jnp.float32), jnp.zeros((B, H, dk), jnp.float32), jnp.zeros((B, H), jnp.float32))
    _, hs = lax.scan(step, init, (chunks(q), chunks(k), chunks(v), chunks(log_i), chunks(log_f)))
    return jnp.moveaxis(hs, 0, 2).reshape(B, H, T, dv)


def mlstm_branch(q, k, v, o, gates, conv_w, conv_b, gate_bias, head_norm_w):
    B, T, _ = q.shape
    H, d = MLSTM_HEADS, MLSTM_HEAD_DIM
    qk = jax.nn.silu(centred_depthwise_conv(jnp.concatenate([q, k], axis=-1), conv_w, conv_b))
    q, k = jnp.split(qk, 2, axis=-1)

    def heads(a):
        return a.reshape(B, T, H, d).transpose(0, 2, 1, 3).astype(jnp.float32)

    qh = heads(q)
    kh = heads(k) * (d ** -0.5)
    vh = heads(v)
    gp = (gates + gate_bias).astype(jnp.float32).reshape(B, T, 4, H).transpose(2, 0, 3, 1)
    i_fwd, f_fwd, i_bwd, f_bwd = gp[0], gp[1], gp[2], gp[3]
    h_fwd = mlstm_chunkwise(qh, kh, vh, i_fwd, jax.nn.log_sigmoid(f_fwd))
    flip = lambda a: jnp.flip(a, axis=2)
    h_bwd = flip(mlstm_chunkwise(flip(qh), flip(kh), flip(vh), jnp.flip(i_bwd, -1), jnp.flip(jax.nn.log_sigmoid(f_bwd), -1)))
    h = (h_fwd + h_bwd).transpose(0, 2, 1, 3)
    mu = jnp.mean(h, axis=-1, keepdims=True)
    var = jnp.mean(jnp.square(h - mu), axis=-1, keepdims=True)
    h = (h - mu) * lax.rsqrt(var + NORM_EPS) * head_norm_w.astype(jnp.float32).reshape(H, d)
    h = jax.nn.sigmoid(o.astype(jnp.float32)) * h.reshape(B, T, MLSTM_WIDTH)
    return h.astype(q.dtype)


def setup_inputs(seed: int = 0) -> dict:
    key = jax.random.key(seed)
    ks = jax.random.split(key, 24)
    f32 = jnp.float32

    def dense(k, fan_in, fan_out):
        return jax.random.normal(k, (DEPTH, fan_in, fan_out), f32) * (fan_in ** -0.5)

    def gain(k, shape):
        return 1.0 + 0.02 * jax.random.normal(k, shape, f32)

    H = MLSTM_HEADS
    i_b = 0.1 * jax.random.normal(ks[9], (DEPTH, 2, H), f32)
    f_b = jnp.linspace(3.0, 6.0, H, dtype=f32)[None, None, :] + 0.1 * jax.random.normal(ks[10], (DEPTH, 2, H), f32)
    gate_bias = jnp.concatenate([i_b[:, 0], f_b[:, 0], i_b[:, 1], f_b[:, 1]], axis=-1)
    return {
        "x": jax.random.normal(ks[0], (BATCH, SEQ, D_MODEL), f32),
        "ffn1_norm": gain(ks[1], (DEPTH, D_MODEL)),
        "ffn1_w_gate": dense(ks[2], D_MODEL, D_FF),
        "ffn1_w_up": dense(ks[3], D_MODEL, D_FF),
        "ffn1_w_down": dense(ks[4], D_FF, D_MODEL),
        "mix_norm": gain(ks[5], (DEPTH, D_MODEL)),
        "w_in": dense(ks[6], D_MODEL, IN_WIDTH),
        "mlstm_conv_w": jax.random.normal(ks[7], (DEPTH, CONV_WIDTH, 2 * MLSTM_WIDTH), f32) * (CONV_WIDTH ** -0.5),
        "mlstm_conv_b": 0.02 * jax.random.normal(ks[8], (DEPTH, 2 * MLSTM_WIDTH), f32),
        "mlstm_gate_bias": gate_bias,
        "mlstm_head_norm": gain(ks[11], (DEPTH, MLSTM_WIDTH)),
        "w_branch_attn": dense(ks[12], ATTN_OUT_WIDTH, D_MODEL),
        "w_branch_mlstm": dense(ks[13], MLSTM_WIDTH, D_MODEL),
        "w_out": dense(ks[14], D_MODEL, D_MODEL),
        "ffn2_norm": gain(ks[15], (DEPTH, D_MODEL)),
        "ffn2_w_gate": dense(ks[16], D_MODEL, D_FF),
        "ffn2_w_up": dense(ks[17], D_MODEL, D_FF),
        "ffn2_w_down": dense(ks[18], D_FF, D_MODEL),
        "final_norm": gain(ks[19], (D_MODEL,)),
    }


def reference(x, ffn1_norm, ffn1_w_gate, ffn1_w_up, ffn1_w_down, mix_norm, w_in, mlstm_conv_w, mlstm_conv_b, mlstm_gate_bias, mlstm_head_norm, w_branch_attn, w_branch_mlstm, w_out, ffn2_norm, ffn2_w_gate, ffn2_w_up, ffn2_w_down, final_norm):
    T = x.shape[1]
    pos = jnp.arange(T)
    for l in range(DEPTH):
        x = x + 0.5 * swiglu(rms_norm(x, ffn1_norm[l]), ffn1_w_gate[l], ffn1_w_up[l], ffn1_w_down[l])
        h = rms_norm(x, mix_norm[l])
        u = h @ w_in[l]
        qa, ka, va, qm, km, vm, om, mgates, g_attn, g_mlstm = split_cols(u, IN_SIZES)
        y_attn = dilated_attention_branch(qa, ka, va, pos)
        y_mlstm = mlstm_branch(qm, km, vm, om, mgates, mlstm_conv_w[l], mlstm_conv_b[l], mlstm_gate_bias[l], mlstm_head_norm[l])
        merged = jax.nn.sigmoid(g_attn) * (y_attn @ w_branch_attn[l]) + jax.nn.sigmoid(g_mlstm) * (y_mlstm @ w_branch_mlstm[l])
        x = x + merged @ w_out[l]
        x = x + 0.5 * swiglu(rms_norm(x, ffn2_norm[l]), ffn2_w_gate[l], ffn2_w_up[l], ffn2_w_down[l])
    return rms_norm(x, final_norm)
```

```python
import numpy as np
import ml_dtypes
import concourse.bass as bass
import concourse.mybir as mybir
from concourse.bass_utils import run_bass_kernel_spmd

F32 = mybir.dt.float32
BF16 = mybir.dt.bfloat16
I32 = mybir.dt.int32
AF = mybir.ActivationFunctionType
ALU = mybir.AluOpType
AX = mybir.AxisListType

NORM_EPS = 1e-6
ROPE_THETA = 500000.0


class Cfg:
    def __init__(self, D=4096, F=11008, T=8192, HA=8, HM=8, B=2):
        self.D, self.F, self.T, self.HA, self.HM, self.B = D, F, T, HA, HM, B
        self.TT = 512
        self.NT = T // 512
        self.DC = D // 128
        self.FC = F // 128
        assert F % 128 == 0 and D % 128 == 0 and T % 1024 == 0
        self.AW = 3 * HA * 128
        self.AO = HA * 128
        self.MW = HM * 256
        o = 0
        self.o_qa = o; o += self.AW
        self.o_ka = o; o += self.AW
        self.o_va = o; o += self.AW
        self.o_qm = o; o += self.MW
        self.o_km = o; o += self.MW
        self.o_vm = o; o += self.MW
        self.o_om = o; o += self.MW
        self.o_mg = o; o += 4 * HM
        self.o_ga = o; o += D
        self.o_gm = o; o += D
        self.IN = o
        ng = -(-self.FC // 11)
        base, rem = divmod(self.FC, ng)
        self.fgroups = []
        s = 0
        for g in range(ng):
            n = base + (1 if g < rem else 0)
            self.fgroups.append((s, n))
            s += n
        self.GMAX = max(n for _, n in self.fgroups)


class Buf:
    __slots__ = ("name", "w", "r", "dsem")

    def __init__(self, name):
        self.name = name
        self.w = {}
        self.r = {}
        self.dsem = None


class Eng:
    def __init__(self, name, sem, idx):
        self.name, self.sem, self.idx = name, sem, idx
        self.cnt = 0
        self.seen = {}
        self.q = []


class Prog:
    def __init__(self, nc, sem_list):
        self.nc = nc
        self.free_sems = list(sem_list)
        self.sem_idx = {}
        self.engs = {}
        for n in ("pe", "act", "dve", "pool", "sp"):
            s = self.free_sems.pop()
            self.sem_idx[id(s)] = len(self.sem_idx)
            self.engs[n] = Eng(n, s, self.sem_idx[id(s)])
        self.dcount = {}
        self.dsems = {}
        self.final = []

    def _need(self, eng, reads, writes):
        need = {}

        def add(d):
            for k, (sem, v) in d.items():
                if k not in need or need[k][1] < v:
                    need[k] = (sem, v)
        for b in reads:
            add(b.w)
        for b in writes:
            add(b.w)
            add(b.r)
        out = []
        for k, (sem, v) in need.items():
            if eng.name == "pe" and k == eng.idx:
                continue
            if eng.seen.get(k, 0) >= v:
                continue
            eng.seen[k] = v
            out.append((sem, v))
        return out

    def _mark(self, tok, key, reads, writes):
        for b in reads:
            if key not in b.r or b.r[key][1] < tok[1]:
                b.r[key] = tok
        for b in writes:
            b.w = {key: tok}
            b.r = {}

    def op(self, en, fn, reads=(), writes=()):
        eng = self.engs[en]
        waits = self._need(eng, reads, writes)
        eng.cnt += 1
        tok = (eng.sem, eng.cnt)
        eng.q.append((waits, fn, eng.sem, 1))
        self._mark(tok, eng.idx, reads, writes)
        return tok

    def dma(self, en, fn, sbuf, reads=(), writes=()):
        eng = self.engs[en]
        if sbuf.dsem is None:
            s = self.free_sems.pop()
            self.sem_idx[id(s)] = len(self.sem_idx)
            sbuf.dsem = s
            self.dcount[id(s)] = 0
            self.dsems[id(s)] = s
        sem = sbuf.dsem
        waits = self._need(eng, reads, writes)
        self.dcount[id(sem)] += 16
        tok = (sem, self.dcount[id(sem)])
        eng.q.append((waits, fn, sem, 16))
        self._mark(tok, self.sem_idx[id(sem)], reads, writes)
        return tok

    def seal(self, owner, bufs, kind="r"):
        sem = owner.dsem
        key = self.sem_idx[id(sem)]
        tok = (sem, self.dcount[id(sem)])
        for b in bufs:
            if kind == "r":
                b.r[key] = tok
            else:
                b.w[key] = tok

    def barrier(self):
        allv = [(e.sem, e.cnt, e.idx) for e in self.engs.values() if e.cnt > 0]
        allv += [(s, self.dcount[i], self.sem_idx[i]) for i, s in self.dsems.items() if self.dcount[i] > 0]
        for eng in self.engs.values():
            waits = []
            for (sem, v, k) in allv:
                if eng.name == "pe" and k == eng.idx:
                    continue
                if eng.seen.get(k, 0) >= v:
                    continue
                eng.seen[k] = v
                waits.append((sem, v))
            eng.q.append((waits, None, None, 0))

    def finish(self, bufs):
        eng = self.engs["sp"]
        waits = self._need(eng, bufs, ())
        eng.q.append((waits, None, None, 0))

    def replay(self, block):
        nc = self.nc
        P = self

        def run(e, q):
            for waits, fn, sem, inc in q:
                for (s, v) in waits:
                    e.wait_ge(s, v)
                if fn is None:
                    continue
                ins = fn(e)
                ins.then_inc(sem, inc)

        @block.tensor
        def _(e):
            run(e, P.engs["pe"].q)

        @block.scalar
        def _(e):
            run(e, P.engs["act"].q)

        @block.vector
        def _(e):
            run(e, P.engs["dve"].q)

        @block.gpsimd
        def _(e):
            run(e, P.engs["pool"].q)

        @block.sync
        def _(e):
            run(e, P.engs["sp"].q)


class Rot:
    def __init__(self, items):
        self.items = items
        self.i = 0

    def next(self):
        it = self.items[self.i % len(self.items)]
        self.i += 1
        return it


def build(cfg, debug=None):
    import contextlib
    c = cfg
    D, F, T, DC, FC, NT = c.D, c.F, c.T, c.DC, c.FC, c.NT
    HA, HM, AW, MW, AO = c.HA, c.HM, c.AW, c.MW, c.AO
    NCH = T // 128
    KA = AO // 128
    KY = (AO + MW) // 128
    NB = D // 256
    GMAX = c.GMAX
    NG = len(c.fgroups)
    SLOT = 4096
    KCAST = 8
    nc = bass.Bass("TRN2", target_bir_lowering=False)
    dbg = {}
    dbg_names = set(debug.split(",")) if isinstance(debug, str) else set()
    order = ["A", "conv", "attn", "m1", "m2", "C"]
    upto = len(order)
    for i_, nm_ in enumerate(order):
        if ("upto_" + nm_) in dbg_names:
            upto = i_ + 1

    def din(name, shape, dt=F32):
        return nc.dram_tensor(name, list(shape), dt, kind="ExternalInput").ap()

    def dscr(name, shape, dt):
        if name in dbg_names:
            t_ = nc.dram_tensor(name, list(shape), dt, kind="ExternalOutput").ap()
            dbg[name] = t_
            return t_
        return nc.dram_tensor(name, list(shape), dt, kind="Internal").ap()

    x_in = din("x", [T, D])
    w_in = {}
    for nm, shp in [("ffn1_norm", [1, D]), ("ffn1_w_gate", [D, F]), ("ffn1_w_up", [D, F]), ("ffn1_w_down", [F, D]),
                    ("mix_norm", [1, D]), ("w_in", [D, c.IN]), ("mlstm_conv_w", [5, 2 * MW]),
                    ("mlstm_conv_b", [1, 2 * MW]), ("mlstm_gate_bias", [1, 4 * HM]),
                    ("mlstm_head_norm", [1, MW]), ("w_branch_attn", [AO, D]), ("w_branch_mlstm", [MW, D]),
                    ("w_out", [D, D]), ("ffn2_norm", [1, D]), ("ffn2_w_gate", [D, F]), ("ffn2_w_up", [D, F]),
                    ("ffn2_w_down", [F, D]), ("final_norm", [1, D])]:
        w_in[nm] = din(nm, shp)
    out = nc.dram_tensor("out", [T, D], F32, kind="ExternalOutput").ap()

    cW = {}
    for l in (1, 2):
        cW[f"g{l}"] = dscr(f"cWg{l}", [FC, 128, DC * 128], BF16)
        cW[f"u{l}"] = dscr(f"cWu{l}", [FC, 128, DC * 128], BF16)
        cW[f"d{l}"] = dscr(f"cWd{l}", [NG * NB, 128, GMAX * 256], BF16)
    s_fams = [("qa", c.o_qa, 3 * HA), ("ka", c.o_ka, 3 * HA), ("va", c.o_va, 3 * HA), ("qm", c.o_qm, 2 * HM),
              ("km", c.o_km, 2 * HM), ("ga", c.o_ga, DC), ("gm", c.o_gm, DC)]
    s_blocks = [(fam, off + 128 * j, j) for (fam, off, n) in s_fams for j in range(n)]
    cWin_s = dscr("cWin_s", [len(s_blocks), 128, DC * 128], BF16)
    KH = min(DC, 8)
    m_blocks = []
    for fam, off, width in (("vm", c.o_vm, MW), ("om", c.o_om, MW), ("mg", c.o_mg, 4 * HM)):
        for j, c0 in enumerate(range(0, width, 512)):
            ncols = min(512, width - c0)
            for k0 in range(0, DC, KH):
                m_blocks.append((fam, off + c0, ncols, k0, min(KH, DC - k0), j))
    cWin_m = dscr("cWin_m", [len(m_blocks), 128, SLOT], BF16)
    cWb = dscr("cWb", [DC, 128, KY * 128], BF16)
    cWout = dscr("cWout", [DC, 128, DC * 128], BF16)
    cWbuf = Buf("cW")

    X1T = dscr("X1T", [D, T], F32)
    QaT = dscr("QaT", [AW, T], BF16)
    KaT = dscr("KaT", [AW, T], BF16)
    VaT = dscr("VaT", [AW, T], BF16)
    QKm = dscr("QKm", [2 * MW, T], F32)
    GT = dscr("GT", [2 * D, T], F32)
    Vm = dscr("Vm", [T, MW], BF16)
    Om = dscr("Om", [T, MW], F32)
    MG = dscr("MG", [T, 4 * HM], F32)
    QmC = dscr("QmC", [NCH, 128, 2 * HM, 128], BF16)
    KmC = dscr("KmC", [NCH, 128, 2 * HM, 128], BF16)
    HRAW = dscr("HRAW", [2, T, HM, 257], F32)
    YT = dscr("YT", [AO + MW, T], BF16)
    B_scr = {k: Buf(k) for k in ("X1T", "QaT", "KaT", "VaT", "QKm", "GT", "Vm", "Om", "MG", "YT", "QmC", "KmC", "HRAW", "out")}

    st = contextlib.ExitStack()
    with st:
        sems = [st.enter_context(nc.semaphore(f"s{i}")) for i in range(96)]
        P = Prog(nc, sems)
        cur = [st]

        sbn = [0]

        def sb(name, shape, dt):
            sbn[0] += 1
            return cur[0].enter_context(nc.sbuf_tensor(f"{name}_{sbn[0]}", list(shape), dt))

        def scr_store(q, fn, sbuf_b, scr_key, owner=None):
            ow = owner if owner is not None else sbuf_b
            if owner is None:
                P.dma(q, fn, sbuf_b, reads=[sbuf_b], writes=[])
            else:
                P.dma(q, fn, owner, reads=[sbuf_b], writes=[])
                P.seal(owner, [sbuf_b], "r")
            B_scr[scr_key].w[P.sem_idx[id(ow.dsem)]] = (ow.dsem, P.dcount[id(ow.dsem)])

        ones_f = sb("ones_f", [128, 128], F32)
        B_ones = Buf("ones_f")
        P.op("pool", lambda e: e.memset(ones_f[:], 1.0), writes=[B_ones])
        ones_b = sb("ones_b", [128, 128], BF16)
        B_onesb = Buf("ones_b")
        P.op("pool", lambda e: e.memset(ones_b[:], 1.0), writes=[B_onesb])
        ident_f = sb("ident_f", [128, 128], F32)
        B_ident = Buf("ident_f")
        P.op("pool", lambda e: e.memset(ident_f[:], 1.0), writes=[B_ident])
        P.op("pool", lambda e: e.affine_select(out=ident_f[:], in_=ident_f[:], pattern=[[-1, 128]], compare_op=ALU.is_equal,
                                               fill=0.0, base=0, channel_multiplier=1), reads=[B_ident], writes=[B_ident])
        ident_b = sb("ident_b", [128, 128], BF16)
        B_identb = Buf("ident_b")
        P.op("pool", lambda e: e.tensor_copy(out=ident_b[:], in_=ident_f[:]), reads=[B_ident], writes=[B_identb])
        gains = sb("gains", [128, 4, DC], F32)
        B_gains = Buf("gains")
        cvt = Rot([(sb(f"cvt{i}", [128, 128], F32), Buf(f"cvt{i}")) for i in range(2)])

        def load_colvec(dst, dst_buf, src_row, n):
            t_, tb = cvt.next()
            P.dma("sp", (lambda e: e.dma_start(out=t_[0:n, :], in_=src_row.rearrange("o (k p) -> (o k) p", p=128))), tb, writes=[tb])
            pb, pbb = bankrot.next()
            P.op("pe", (lambda e: e.transpose(pb[:, 0:n], t_[0:n, :], ident_f[0:n, 0:n])), reads=[tb, B_ident], writes=[pbb])
            P.op("dve", (lambda e: e.tensor_copy(out=dst, in_=pb[:, 0:n])), reads=[pbb, dst_buf], writes=[dst_buf])

        banks = []
        for i in range(8):
            t_ = st.enter_context(nc.psum_tensor(f"bank{i}", [128, 512], F32))
            banks.append((t_, Buf(f"bank{i}")))
        bankrot = Rot(banks)
        for i, nm in enumerate(["ffn1_norm", "mix_norm", "ffn2_norm", "final_norm"]):
            load_colvec(gains[:, i, :], B_gains, w_in[nm], DC)

        evac_rr = [0]

        def copy_evac(dst, src, reads, writes):
            en = ("act", "dve")[evac_rr[0] % 2]
            evac_rr[0] += 1
            if en == "act":
                P.op("act", (lambda e: e.copy(out=dst, in_=src)), reads=reads, writes=writes)
            else:
                P.op("dve", (lambda e: e.tensor_copy(out=dst, in_=src)), reads=reads, writes=writes)

        def prepass():
            with contextlib.ExitStack() as pst:
                cur[0] = pst
                stg = Rot([(sb(f"stg{i}", [128, SLOT], F32), Buf(f"stg{i}")) for i in range(3)])
                bfs = Rot([(sb(f"bfs{i}", [128, SLOT], BF16), Buf(f"bfs{i}")) for i in range(3)])
                rr = [0]

                def cast_block(src3, nk, ncols, dst2):
                    ne = nk * ncols
                    assert ne <= SLOT
                    s_t, s_b = stg.next()
                    b_t, b_b = bfs.next()
                    P.dma("sp", (lambda e: e.dma_start(out=s_t[:, 0:ne].rearrange("p (k c) -> p k c", k=nk), in_=src3)),
                          s_b, writes=[s_b])
                    en = ("act", "dve", "pool")[rr[0] % 3]
                    rr[0] += 1
                    if en == "act":
                        P.op("act", (lambda e: e.copy(out=b_t[:, 0:ne], in_=s_t[:, 0:ne])), reads=[s_b], writes=[b_b])
                    else:
                        P.op(en, (lambda e: e.tensor_copy(out=b_t[:, 0:ne], in_=s_t[:, 0:ne])), reads=[s_b], writes=[b_b])
                    P.dma("pool", (lambda e: e.dma_start(out=dst2, in_=b_t[:, 0:ne])), b_b, reads=[b_b], writes=[])
                    cWbuf.w[P.sem_idx[id(b_b.dsem)]] = (b_b.dsem, P.dcount[id(b_b.dsem)])

                for l in (1, 2):
                    wg = w_in[f"ffn{l}_w_gate"].rearrange("(k p) n -> p k n", p=128)
                    wu = w_in[f"ffn{l}_w_up"].rearrange("(k p) n -> p k n", p=128)
                    wd = w_in[f"ffn{l}_w_down"].rearrange("(k p) n -> p k n", p=128)
                    for fc in range(FC):
                        for (w3, key) in ((wg, f"g{l}"), (wu, f"u{l}")):
                            for k0 in range(0, DC, KCAST):
                                nk = min(KCAST, DC - k0)
                                cast_block(w3[:, k0:k0 + nk, fc * 128:(fc + 1) * 128], nk, 128,
                                           cW[key][fc, :, k0 * 128:(k0 + nk) * 128])
                    for g, (f0, n) in enumerate(c.fgroups):
                        for b in range(NB):
                            cast_block(wd[:, f0:f0 + n, b * 256:(b + 1) * 256], n, 256, cW[f"d{l}"][g * NB + b, :, 0:n * 256])
                w3 = w_in["w_in"].rearrange("(k p) n -> p k n", p=128)
                for bi, (fam, off, j) in enumerate(s_blocks):
                    for k0 in range(0, DC, KCAST):
                        nk = min(KCAST, DC - k0)
                        cast_block(w3[:, k0:k0 + nk, off:off + 128], nk, 128, cWin_s[bi, :, k0 * 128:(k0 + nk) * 128])
                for bi, (fam, off, ncols, k0, nk, j) in enumerate(m_blocks):
                    cast_block(w3[:, k0:k0 + nk, off:off + ncols], nk, ncols, cWin_m[bi, :, 0:nk * ncols])
                wa3 = w_in["w_branch_attn"].rearrange("(k p) n -> p k n", p=128)
                wm3 = w_in["w_branch_mlstm"].rearrange("(k p) n -> p k n", p=128)
                wo3 = w_in["w_out"].rearrange("(k p) n -> p k n", p=128)
                for dc in range(DC):
                    cast_block(wa3[:, :, dc * 128:(dc + 1) * 128], KA, 128, cWb[dc, :, 0:KA * 128])
                    cast_block(wm3[:, :, dc * 128:(dc + 1) * 128], KY - KA, 128, cWb[dc, :, KA * 128:KY * 128])
                    for k0 in range(0, DC, KCAST):
                        nk = min(KCAST, DC - k0)
                        cast_block(wo3[:, k0:k0 + nk, dc * 128:(dc + 1) * 128], nk, 128, cWout[dc, :, k0 * 128:(k0 + nk) * 128])
                P.barrier()
            cur[0] = st

        class Core:
            pass

        def make_core(NS):
            K = Core()
            wslots = [(sb(f"wslot{i}", [128, SLOT], BF16), Buf(f"wslot{i}")) for i in range(NS)]

            class WQ:
                def __init__(self):
                    self.plan = []
                    self.k = 0
                    self.loaded = 0

                def add(self, src_ap, nelem):
                    self.plan.append((src_ap, nelem))

                def get_group(self, n):
                    idx = self.k
                    self.k += n
                    assert n <= NS
                    lim = min(len(self.plan), idx + NS)
                    while self.loaded < lim:
                        i = self.loaded
                        src, ne = self.plan[i]
                        tl, bf = wslots[i % NS]
                        P.dma("sp", (lambda e, tl=tl, src=src, ne=ne: e.dma_start(out=tl[:, 0:ne], in_=src)),
                              bf, reads=[cWbuf], writes=[bf])
                        self.loaded += 1
                    return [wslots[(idx + q) % NS] for q in range(n)]

                def get(self):
                    return self.get_group(1)[0]
            wq = WQ()
            K.wq = wq
            xT = sb("xT", [128, DC, 512], F32)
            B_xT = [Buf(f"xT{i}") for i in range(DC)]
            hT = sb("hT", [128, DC, 512], BF16)
            B_hT = [Buf(f"hT{i}") for i in range(DC)]
            actT = sb("actT", [128, GMAX, 512], BF16)
            B_act = [Buf(f"act{i}") for i in range(GMAX)]
            fpool = Rot([(sb(f"fp{i}", [128, 512], F32), Buf(f"fp{i}")) for i in range(7)])
            rstd_bc = sb("rstd_bc", [128, 512], F32)
            B_rstd = Buf("rstd")
            K.xT, K.B_xT, K.hT, K.B_hT, K.fpool = xT, B_xT, hT, B_hT, fpool
            K.rstd_bc, K.B_rstd = rstd_bc, B_rstd

            def rms_stats():
                pb, pbb = bankrot.next()
                for dc in range(DC):
                    t_, tb = fpool.next()
                    P.op("act", (lambda e, dc=dc, t_=t_: e.activation(out=t_[:], in_=xT[:, dc, :], func=AF.Square)),
                         reads=[B_xT[dc]], writes=[tb])
                    P.op("pe", (lambda e, dc=dc, t_=t_: e.matmul(pb[:], lhsT=ones_f[:], rhs=t_[:], start=(dc == 0), stop=(dc == DC - 1))),
                         reads=[tb, B_ones], writes=[pbb])
                t1, t1b = fpool.next()
                P.op("dve", (lambda e: e.tensor_scalar(out=t1[:], in0=pb[:], scalar1=1.0 / D, scalar2=NORM_EPS, op0=ALU.mult, op1=ALU.add)),
                     reads=[pbb], writes=[t1b])
                t2, t2b = fpool.next()
                P.op("act", (lambda e: e.activation(out=t2[:], in_=t1[:], func=AF.Sqrt)), reads=[t1b], writes=[t2b])
                P.op("dve", (lambda e: e.reciprocal(out=rstd_bc[:], in_=t2[:])), reads=[t2b], writes=[B_rstd])

            def rmsnorm_T(gi):
                rms_stats()
                for dc in range(DC):
                    P.op("dve", (lambda e, dc=dc: e.scalar_tensor_tensor(out=hT[:, dc, :], in0=xT[:, dc, :], scalar=gains[:, gi, dc:dc + 1],
                                                                          in1=rstd_bc[:], op0=ALU.mult, op1=ALU.mult)),
                         reads=[B_xT[dc], B_gains, B_rstd], writes=[B_hT[dc]])
            K.rms_stats, K.rmsnorm_T = rms_stats, rmsnorm_T

            def plan_ffn(l):
                for g, (f0, n) in enumerate(c.fgroups):
                    for j in range(n):
                        wq.add(cW[f"g{l}"][f0 + j, :, :], DC * 128)
                        wq.add(cW[f"u{l}"][f0 + j, :, :], DC * 128)
                    for b in range(NB):
                        wq.add(cW[f"d{l}"][g * NB + b, :, 0:n * 256], n * 256)

            def ffn(l):
                for g, (f0, n) in enumerate(c.fgroups):
                    for j in range(n):
                        pg, pgb = bankrot.next()
                        pu, pub = bankrot.next()
                        for (pt, ptb) in ((pg, pgb), (pu, pub)):
                            wt, wb = wq.get()

                            def mm(e, wt=wt, pt=pt):
                                ins = None
                                for dc in range(DC):
                                    ins = e.matmul(pt[:], lhsT=wt[:, dc * 128:(dc + 1) * 128], rhs=hT[:, dc, :],
                                                   start=(dc == 0), stop=(dc == DC - 1))
                                return ins
                            P.op("pe", mm, reads=[wb] + B_hT, writes=[ptb])
                        t_, tb = fpool.next()
                        P.op("act", (lambda e, t_=t_, pg=pg: e.activation(out=t_[:], in_=pg[:], func=AF.Silu)), reads=[pgb], writes=[tb])
                        P.op("dve", (lambda e, t_=t_, pu=pu, j=j: e.tensor_tensor(out=actT[:, j, :], in0=t_[:], in1=pu[:], op=ALU.mult)),
                             reads=[tb, pub], writes=[B_act[j]])
                    for b in range(NB):
                        wt, wb = wq.get()
                        pp = [bankrot.next(), bankrot.next()]

                        def mm(e, wt=wt, pp=pp, n=n):
                            ins = None
                            for dl in range(2):
                                for jj in range(n):
                                    ins = e.matmul(pp[dl][0][:], lhsT=wt[:, jj * 256 + dl * 128: jj * 256 + (dl + 1) * 128],
                                                   rhs=actT[:, jj, :], start=(jj == 0), stop=(jj == n - 1))
                            return ins
                        P.op("pe", mm, reads=[wb] + B_act[:n], writes=[pp[0][1], pp[1][1]])
                        for dl in range(2):
                            dc = 2 * b + dl
                            P.op("dve", (lambda e, dc=dc, pt=pp[dl][0]: e.scalar_tensor_tensor(out=xT[:, dc, :], in0=pt[:], scalar=0.5,
                                                                                               in1=xT[:, dc, :], op0=ALU.mult, op1=ALU.add)),
                                 reads=[pp[dl][1], B_xT[dc]], writes=[B_xT[dc]])
            K.plan_ffn, K.ffn = plan_ffn, ffn
            return K

        def phaseA():
            with contextlib.ExitStack() as pst:
                cur[0] = pst
                K = make_core(6)
                xT, B_xT, hT, B_hT, fpool, wq = K.xT, K.B_xT, K.hT, K.B_hT, K.fpool, K.wq
                XH = min(D, 2048)
                xtok = Rot([(sb(f"xtok{i}", [128, XH], F32), Buf(f"xtok{i}")) for i in range(2)])

                def load_xT(i):
                    for tcn in range(4):
                        r0 = i * 512 + tcn * 128
                        for h0 in range(0, D, XH):
                            xt, xb = xtok.next()
                            P.dma("sp", (lambda e, xt=xt, r0=r0, h0=h0: e.dma_start(out=xt[:], in_=x_in[r0:r0 + 128, h0:h0 + XH])), xb, writes=[xb])
                            for q0 in range(0, XH // 128, 4):
                                pb, pbb = bankrot.next()
                                nd = min(4, XH // 128 - q0)
                                dc0 = h0 // 128 + q0

                                def tr(e, xt=xt, pb=pb, q0=q0, nd=nd):
                                    ins = None
                                    for q in range(nd):
                                        ins = e.transpose(pb[:, q * 128:(q + 1) * 128], xt[:, (q0 + q) * 128:(q0 + q + 1) * 128], ident_f[:])
                                    return ins
                                P.op("pe", tr, reads=[xb, B_ident], writes=[pbb])
                                dst = xT[:, dc0:dc0 + nd, tcn * 128:(tcn + 1) * 128]
                                src = pb[:, 0:nd * 128].rearrange("p (q t) -> p q t", q=nd)
                                copy_evac(dst, src, [pbb], B_xT[dc0:dc0 + nd])

                B_xTst = Buf("xTstore")

                def store_xT(dst, key, i):
                    for dc in range(DC):
                        P.dma("pool", (lambda e, dc=dc: e.dma_start(out=dst[dc * 128:(dc + 1) * 128, i * 512:(i + 1) * 512], in_=xT[:, dc, :])),
                              B_xTst, reads=[B_xT[dc]], writes=[])
                    P.seal(B_xTst, B_xT, "r")
                    B_scr[key].w[P.sem_idx[id(B_xTst.dsem)]] = (B_xTst.dsem, P.dcount[id(B_xTst.dsem)])

                rc = sb("rope_c", [32, 4], F32)
                B_rc = Buf("rope_c")
                perm = sb("perm", [32, 32], F32)
                B_perm = Buf("perm")
                permb = sb("permb", [32, 32], F32)
                B_permb = Buf("permb")
                P.op("pool", lambda e: e.iota(out=rc[:, 2:3], pattern=[[0, 1]], base=0, channel_multiplier=1,
                                              allow_small_or_imprecise_dtypes=True), writes=[B_rc])
                P.op("dve", lambda e: e.tensor_scalar(out=rc[:, 3:4], in0=rc[:, 2:3], scalar1=15.5, scalar2=None, op0=ALU.is_gt),
                     reads=[B_rc], writes=[B_rc])
                P.op("dve", lambda e: e.tensor_scalar(out=rc[:, 1:2], in0=rc[:, 3:4], scalar1=2.0, scalar2=-1.0, op0=ALU.mult, op1=ALU.add),
                     reads=[B_rc], writes=[B_rc])
                P.op("dve", lambda e: e.scalar_tensor_tensor(out=rc[:, 2:3], in0=rc[:, 3:4], scalar=-16.0, in1=rc[:, 2:3], op0=ALU.mult, op1=ALU.add),
                     reads=[B_rc], writes=[B_rc])
                P.op("act", lambda e: e.activation(out=rc[:, 0:1], in_=rc[:, 2:3], func=AF.Exp, scale=-float(np.log(ROPE_THETA)) / 16.0),
                     reads=[B_rc], writes=[B_rc])
                P.op("pool", lambda e: e.memset(perm[:], 1.0), writes=[B_perm])
                P.op("pool", lambda e: e.memset(permb[:], 1.0), writes=[B_permb])
                P.op("pool", lambda e: e.affine_select(out=perm[:], in_=perm[:], pattern=[[-1, 32]], compare_op=ALU.is_equal,
                                                       fill=0.0, base=-16, channel_multiplier=1), reads=[B_perm], writes=[B_perm])
                P.op("pool", lambda e: e.affine_select(out=permb[:], in_=permb[:], pattern=[[-1, 32]], compare_op=ALU.is_equal,
                                                       fill=0.0, base=16, channel_multiplier=1), reads=[B_permb], writes=[B_permb])
                P.op("pool", lambda e: e.tensor_tensor(out=perm[:], in0=perm[:], in1=permb[:], op=ALU.add), reads=[B_perm, B_permb], writes=[B_perm])

                cossin = sb("cossin", [32, 2, 512], F32)
                cos32 = cossin[:, 0, :]
                sin32 = cossin[:, 1, :]
                B_cs = Buf("cossin")
                rti = sb("ropeti", [32, 512], I32)
                B_rt = Buf("ropetmp")
                TWO_PI = float(2 * np.pi)

                def rope_tables(i):
                    (u_, ub_), (a_, ab_), (b__, bb_) = fpool.next(), fpool.next(), fpool.next()
                    u, a, b_ = u_[0:32, :], a_[0:32, :], b__[0:32, :]
                    R, W_ = [B_rt, B_rc, ub_, ab_, bb_], [B_rt, ub_, ab_, bb_]
                    P.op("pool", lambda e: e.iota(out=u, pattern=[[1, 512]], base=i * 512, channel_multiplier=0,
                                                  allow_small_or_imprecise_dtypes=True), reads=R, writes=W_)
                    P.op("dve", lambda e: e.tensor_scalar(out=u, in0=u, scalar1=rc[:, 0:1], scalar2=1.0 / TWO_PI, op0=ALU.mult, op1=ALU.mult), reads=R, writes=W_)
                    P.op("dve", lambda e: e.tensor_copy(out=rti[:], in_=u), reads=R, writes=W_)
                    P.op("dve", lambda e: e.tensor_copy(out=a, in_=rti[:]), reads=R, writes=W_)
                    P.op("dve", lambda e: e.tensor_tensor(out=u, in0=u, in1=a, op=ALU.subtract), reads=R, writes=W_)

                    def wrap(t):
                        P.op("dve", lambda e: e.tensor_scalar(out=a, in0=t, scalar1=0.5, scalar2=None, op0=ALU.is_gt), reads=R, writes=W_)
                        P.op("dve", lambda e: e.tensor_tensor(out=t, in0=t, in1=a, op=ALU.subtract), reads=R, writes=W_)
                        P.op("dve", lambda e: e.tensor_scalar(out=a, in0=t, scalar1=-0.5, scalar2=None, op0=ALU.is_lt), reads=R, writes=W_)
                        P.op("dve", lambda e: e.tensor_tensor(out=t, in0=t, in1=a, op=ALU.add), reads=R, writes=W_)
                    wrap(u)
                    P.op("act", lambda e: e.activation(out=sin32, in_=u, func=AF.Sin, scale=TWO_PI), reads=R + [B_cs], writes=[B_cs])
                    P.op("dve", lambda e: e.tensor_scalar(out=sin32, in0=sin32, scalar1=rc[:, 1:2], scalar2=None, op0=ALU.mult), reads=[B_cs, B_rc], writes=[B_cs])
                    P.op("dve", lambda e: e.tensor_scalar(out=b_, in0=u, scalar1=0.25, scalar2=None, op0=ALU.add), reads=R, writes=W_)
                    wrap(b_)
                    P.op("act", lambda e: e.activation(out=cos32, in_=b_, func=AF.Sin, scale=TWO_PI), reads=R + [B_cs], writes=[B_cs])

                stb = Rot([(sb(f"stb{i}", [128, 512], BF16), Buf(f"stb{i}")) for i in range(4)])
                q32r = Rot([(sb(f"q32_{i}", [32, 512], F32), Buf(f"q32_{i}")) for i in range(2)])

                def plan_win():
                    for bi in range(len(s_blocks)):
                        wq.add(cWin_s[bi, :, :], DC * 128)
                    for bi, (fam, off, ncols, k0, nk, j) in enumerate(m_blocks):
                        wq.add(cWin_m[bi, :, 0:nk * ncols], nk * ncols)

                def inproj(i):
                    cols = slice(i * 512, (i + 1) * 512)
                    for bi, (fam, off, j) in enumerate(s_blocks):
                        wt, wb = wq.get()
                        pb, pbb = bankrot.next()

                        def mm(e, wt=wt, pb=pb):
                            ins = None
                            for dc in range(DC):
                                ins = e.matmul(pb[:], lhsT=wt[:, dc * 128:(dc + 1) * 128], rhs=hT[:, dc, :], start=(dc == 0), stop=(dc == DC - 1))
                            return ins
                        if "nostat" in dbg_names or (("norot" in dbg_names) and fam in ("qa", "ka")):
                            continue
                        P.op("pe", mm, reads=[wb] + B_hT, writes=[pbb])
                        if fam in ("qa", "ka"):
                            o_t, o_b = stb.next()
                            q_t, q_b = q32r.next()
                            P.op("act", (lambda e, o_t=o_t, pb=pb: e.copy(out=o_t[:], in_=pb[:])), reads=[pbb], writes=[o_b])
                            P.op("dve", (lambda e, q_t=q_t, pb=pb: e.tensor_copy(out=q_t[:], in_=pb[0:32, :])), reads=[pbb], writes=[q_b])
                            t_, tb = fpool.next()
                            P.dma("pool", (lambda e, t_=t_, q_t=q_t: e.dma_start(out=t_[0:16, :], in_=q_t[16:32, :])), tb, reads=[q_b], writes=[tb])
                            P.dma("pool", (lambda e, t_=t_, q_t=q_t: e.dma_start(out=t_[16:32, :], in_=q_t[0:16, :])), tb, reads=[q_b], writes=[tb])
                            P.op("dve", (lambda e, t_=t_: e.tensor_tensor(out=t_[0:32, :], in0=t_[0:32, :], in1=sin32, op=ALU.mult)),
                                 reads=[tb, B_cs], writes=[tb])
                            P.op("dve", (lambda e, q_t=q_t: e.tensor_tensor(out=q_t[:], in0=q_t[:], in1=cos32, op=ALU.mult)),
                                 reads=[q_b, B_cs], writes=[q_b])
                            P.op("dve", (lambda e, o_t=o_t, q_t=q_t, t_=t_: e.tensor_tensor(out=o_t[0:32, :], in0=q_t[:], in1=t_[0:32, :], op=ALU.add)),
                                 reads=[q_b, tb], writes=[o_b])
                            dstT = QaT if fam == "qa" else KaT
                            scr_store("pool", (lambda e, o_t=o_t, dstT=dstT, j=j: e.dma_start(out=dstT[j * 128:(j + 1) * 128, cols], in_=o_t[:])),
                                      o_b, "QaT" if fam == "qa" else "KaT")
                        elif fam == "va":
                            o_t, o_b = stb.next()
                            copy_evac(o_t[:], pb[:], [pbb], [o_b])
                            scr_store("pool", (lambda e, o_t=o_t, j=j: e.dma_start(out=VaT[j * 128:(j + 1) * 128, cols], in_=o_t[:])), o_b, "VaT")
                        elif fam in ("qm", "km"):
                            o_t, o_b = fpool.next()
                            copy_evac(o_t[:], pb[:], [pbb], [o_b])
                            r0 = j * 128 + (MW if fam == "km" else 0)
                            scr_store("pool", (lambda e, o_t=o_t, r0=r0: e.dma_start(out=QKm[r0:r0 + 128, cols], in_=o_t[:])), o_b, "QKm")
                        else:
                            o_t, o_b = fpool.next()
                            P.op("act", (lambda e, o_t=o_t, pb=pb: e.activation(out=o_t[:], in_=pb[:], func=AF.Sigmoid)), reads=[pbb], writes=[o_b])
                            r0 = j * 128 + (D if fam == "gm" else 0)
                            scr_store("pool", (lambda e, o_t=o_t, r0=r0: e.dma_start(out=GT[r0:r0 + 128, cols], in_=o_t[:])), o_b, "GT")
                    mi = 0
                    while mi < len(m_blocks):
                        fam, off, ncols, k0, nk, j = m_blocks[mi]
                        grp = [m_blocks[mi]]
                        while mi + len(grp) < len(m_blocks) and m_blocks[mi + len(grp)][0] == fam and m_blocks[mi + len(grp)][5] == j:
                            grp.append(m_blocks[mi + len(grp)])
                        assert len(grp) <= 5
                        slots = wq.get_group(len(grp))
                        for tcn in range(4 if "nomov" not in dbg_names else 0):
                            pb, pbb = bankrot.next()

                            def mm(e, pb=pb, tcn=tcn, grp=grp, slots=slots, ncols=ncols):
                                ins = None
                                for blk, (wt, wb) in zip(grp, slots):
                                    _, _, _, k0_, nk_, _ = blk
                                    for kk in range(nk_):
                                        dc = k0_ + kk
                                        ins = e.matmul(pb[:, 0:ncols], lhsT=hT[:, dc, tcn * 128:(tcn + 1) * 128],
                                                       rhs=wt[:, kk * ncols:(kk + 1) * ncols], start=(dc == 0), stop=(dc == DC - 1))
                                return ins
                            P.op("pe", mm, reads=[wb for (_, wb) in slots] + B_hT, writes=[pbb])
                            rows = slice(i * 512 + tcn * 128, i * 512 + (tcn + 1) * 128)
                            c0 = j * 512
                            if fam == "vm":
                                o_t, o_b = stb.next()
                                copy_evac(o_t[:, 0:ncols], pb[:, 0:ncols], [pbb], [o_b])
                                scr_store("pool", (lambda e, o_t=o_t, rows=rows, c0=c0, ncols=ncols: e.dma_start(out=Vm[rows, c0:c0 + ncols], in_=o_t[:, 0:ncols])), o_b, "Vm")
                            elif fam == "om":
                                o_t, o_b = fpool.next()
                                P.op("act", (lambda e, o_t=o_t, pb=pb, ncols=ncols: e.activation(out=o_t[:, 0:ncols], in_=pb[:, 0:ncols], func=AF.Sigmoid)),
                                     reads=[pbb], writes=[o_b])
                                scr_store("pool", (lambda e, o_t=o_t, rows=rows, c0=c0, ncols=ncols: e.dma_start(out=Om[rows, c0:c0 + ncols], in_=o_t[:, 0:ncols])), o_b, "Om")
                            else:
                                o_t, o_b = fpool.next()
                                copy_evac(o_t[:, 0:ncols], pb[:, 0:ncols], [pbb], [o_b])
                                scr_store("pool", (lambda e, o_t=o_t, rows=rows, ncols=ncols: e.dma_start(out=MG[rows, 0:ncols], in_=o_t[:, 0:ncols])), o_b, "MG")
                        mi += len(grp)

                for i in range(NT):
                    K.plan_ffn(1)
                    plan_win()
                lvl = 9
                for q_ in range(9):
                    if f"pa{q_}" in dbg_names:
                        lvl = q_
                for i in range(NT):
                    load_xT(i)
                    if lvl >= 2:
                        K.rmsnorm_T(0)
                    if lvl >= 3:
                        K.ffn(1)
                    store_xT(X1T, "X1T", i)
                    if lvl >= 4:
                        K.rmsnorm_T(1)
                    if lvl >= 5:
                        rope_tables(i)
                    if lvl >= 6:
                        inproj(i)
                P.barrier()
            cur[0] = st

        def conv_pass():
            with contextlib.ExitStack() as pst:
                cur[0] = pst
                CT = 1024
                NCC = 2 * MW // 128
                cwt = sb("cwt", [128, NCC, 5], F32)
                cbt = sb("cbt", [128, NCC], F32)
                B_cw = Buf("cw")
                for j in range(5):
                    load_colvec(cwt[:, :, j], B_cw, w_in["mlstm_conv_w"][j:j + 1, :], NCC)
                load_colvec(cbt[:], B_cw, w_in["mlstm_conv_b"], NCC)
                U = Rot([(sb(f"cu{i}", [128, CT + 4], F32), Buf(f"cu{i}")) for i in range(2)])
                ACC = Rot([(sb(f"ca{i}", [128, CT], F32), Buf(f"ca{i}")) for i in range(2)])
                OUT = Rot([(sb(f"co{i}", [128, CT], BF16), Buf(f"co{i}")) for i in range(2)])
                QmCv = QmC.rearrange("c p k t -> p c k t")
                KmCv = KmC.rearrange("c p k t -> p c k t")
                for cc in range(NCC):
                    is_k = cc >= MW // 128
                    kidx = cc - (MW // 128 if is_k else 0)
                    for t0 in range(0, T, CT):
                        ut, ub = U.next()
                        lo, hi = max(t0 - 2, 0), min(t0 + CT + 2, T)
                        if t0 == 0:
                            P.op("pool", (lambda e, ut=ut: e.memset(ut[:, 0:2], 0.0)), writes=[ub])
                        if t0 + CT == T:
                            P.op("pool", (lambda e, ut=ut: e.memset(ut[:, CT + 2:CT + 4], 0.0)), writes=[ub])
                        P.dma("sp", (lambda e, ut=ut, lo=lo, hi=hi, t0=t0, cc=cc: e.dma_start(out=ut[:, lo - (t0 - 2):hi - (t0 - 2)],
                                                                                             in_=QKm[cc * 128:(cc + 1) * 128, lo:hi])),
                              ub, reads=[B_scr["QKm"]], writes=[ub])
                        at, ab = ACC.next()
                        P.op("dve", (lambda e, at=at, ut=ut, cc=cc: e.tensor_scalar(out=at[:], in0=ut[:, 0:CT], scalar1=cwt[:, cc, 0:1],
                                                                                    scalar2=cbt[:, cc:cc + 1], op0=ALU.mult, op1=ALU.add)),
                             reads=[ub, B_cw], writes=[ab])
                        for j in range(1, 5):
                            P.op("dve", (lambda e, at=at, ut=ut, cc=cc, j=j: e.scalar_tensor_tensor(out=at[:], in0=ut[:, j:j + CT], scalar=cwt[:, cc, j:j + 1],
                                                                                                      in1=at[:], op0=ALU.mult, op1=ALU.add)),
                                 reads=[ub, B_cw, ab], writes=[ab])
                        ot, ob = OUT.next()
                        if not is_k:
                            P.op("act", (lambda e, at=at, ot=ot: e.activation(out=ot[:], in_=at[:], func=AF.Silu)), reads=[ab], writes=[ob])
                        else:
                            P.op("act", (lambda e, at=at: e.activation(out=at[:], in_=at[:], func=AF.Silu)), reads=[ab], writes=[ab])
                            P.op("pool", (lambda e, at=at, ot=ot: e.tensor_scalar(out=ot[:], in0=at[:], scalar1=1.0 / 16.0, scalar2=None, op0=ALU.mult)),
                                 reads=[ab], writes=[ob])
                        dstv = (KmCv if is_k else QmCv)[:, t0 // 128:(t0 + CT) // 128, kidx, :]
                        scr_store("pool", (lambda e, ot=ot, dstv=dstv: e.dma_start(out=dstv, in_=ot[:].rearrange("p (c t) -> p c t", t=128))),
                                  ob, "KmC" if is_k else "QmC")
                P.barrier()
            cur[0] = st

        def attention():
            with contextlib.ExitStack() as pst:
                cur[0] = pst
                mask3 = sb("mask3", [128, 3, 128], BF16)
                mtmp = sb("mtmp", [128, 3, 128], F32)
                B_mask = Buf("mask3")
                P.op("pool", lambda e: e.memset(mtmp[:], 1.0), writes=[B_mask])
                P.op("pool", lambda e: e.affine_select(out=mtmp[:, 0, :], in_=mtmp[:, 0, :], pattern=[[-1, 128]], compare_op=ALU.is_ge,
                                                       fill=0.0, base=-64, channel_multiplier=1), reads=[B_mask], writes=[B_mask])
                P.op("pool", lambda e: e.affine_select(out=mtmp[:, 1, :], in_=mtmp[:, 1, :], pattern=[[-1, 128]], compare_op=ALU.is_ge,
                                                       fill=0.0, base=64, channel_multiplier=1), reads=[B_mask], writes=[B_mask])
                P.op("pool", lambda e: e.affine_select(out=mtmp[:, 1, :], in_=mtmp[:, 1, :], pattern=[[1, 128]], compare_op=ALU.is_ge,
                                                       fill=0.0, base=64, channel_multiplier=-1), reads=[B_mask], writes=[B_mask])
                P.op("pool", lambda e: e.affine_select(out=mtmp[:, 2, :], in_=mtmp[:, 2, :], pattern=[[1, 128]], compare_op=ALU.is_ge,
                                                       fill=0.0, base=-64, channel_multiplier=-1), reads=[B_mask], writes=[B_mask])
                P.op("pool", lambda e: e.tensor_copy(out=mask3[:], in_=mtmp[:]), reads=[B_mask], writes=[B_mask])
                QKV = Rot([((sb(f"aq{i}", [128, T], BF16), sb(f"ak{i}", [128, T], BF16), sb(f"av{i}", [128, T], BF16)), Buf(f"aqkv{i}")) for i in range(2)])
                ACC = sb("aacc", [128, 2, T], F32)
                B_acc = Buf("aacc")
                Vrm = sb("vrm", [128, T // 128, 128], BF16)
                B_vrm = Buf("vrm")
                Pt = Rot([(sb(f"apt{i}", [128, 3, 128], BF16), Buf(f"apt{i}")) for i in range(3)])
                yout = sb("ayout", [128, T], BF16)
                B_yout = Buf("ayout")
                scale = float(128 ** -0.5)
                for h in range(HA):
                    for g, d in enumerate((1, 4, 16)):
                        hh = g * HA + h
                        (qt, kt, vt), qb = QKV.next()
                        for tl, src in ((qt, QaT), (kt, KaT), (vt, VaT)):
                            P.dma("sp", (lambda e, tl=tl, src=src, hh=hh: e.dma_start(out=tl[:], in_=src[hh * 128:(hh + 1) * 128, :])),
                                  qb, reads=[B_scr["QaT"], B_scr["KaT"], B_scr["VaT"]], writes=[qb])
                        S = T // d
                        nb = S // 128

                        def cols(r, b, d=d):
                            s0 = r + d * 128 * b
                            return slice(s0, s0 + 127 * d + 1, d)
                        allc = [(r, b) for r in range(d) for b in range(nb)]
                        for c4 in range(0, len(allc), 4):
                            pb, pbb = bankrot.next()
                            pbv = pb[:].bitcast(BF16)

                            def tr(e, c4=c4, pbv=pbv, vt=vt, allc=allc, cols=cols):
                                ins = None
                                for q in range(4):
                                    r, b = allc[c4 + q]
                                    ins = e.transpose(pbv[:, q * 128:(q + 1) * 128], vt[:, cols(r, b)], ident_b[:])
                                return ins
                            P.op("pe", tr, reads=[qb, B_identb], writes=[pbb])
                            copy_evac(Vrm[:, c4:c4 + 4, :], pbv[:, 0:512].rearrange("p (q t) -> p q t", q=4), [pbb], [B_vrm])

                        def s1(r, b, qt=qt, kt=kt, nb=nb, cols=cols):
                            cks = [cb for cb in (b - 1, b, b + 1) if 0 <= cb < nb]
                            j0, nj = cks[0] - b + 1, len(cks)
                            pS, pSb = bankrot.next()

                            def mm(e):
                                ins = None
                                for cb in cks:
                                    j = cb - b + 1
                                    ins = e.matmul(pS[:, j * 128:(j + 1) * 128], lhsT=kt[:, cols(r, cb)], rhs=qt[:, cols(r, b)], start=True, stop=True)
                                return ins
                            P.op("pe", mm, reads=[qb], writes=[pSb])
                            pt, ptb = Pt.next()
                            P.op("act", (lambda e: e.activation(out=pt[:, j0:j0 + nj, :], in_=pS[:, j0 * 128:(j0 + nj) * 128].rearrange("p (j t) -> p j t", t=128),
                                                                func=AF.Exp, scale=scale)), reads=[pSb], writes=[ptb])
                            P.op("pool", (lambda e: e.tensor_tensor(out=pt[:, j0:j0 + nj, :], in0=pt[:, j0:j0 + nj, :], in1=mask3[:, j0:j0 + nj, :], op=ALU.mult)),
                                 reads=[ptb, B_mask], writes=[ptb])
                            return (r, b, cks, pt, ptb)

                        def s2(stt, g=g, nb=nb, cols=cols):
                            r, b, cks, pt, ptb = stt
                            pO, pOb = bankrot.next()

                            def mm(e):
                                ins = None
                                for ii, cb in enumerate(cks):
                                    j = cb - b + 1
                                    ins = e.matmul(pO[:, 0:128], lhsT=Vrm[:, r * nb + cb, :], rhs=pt[:, j, :], start=(ii == 0), stop=(ii == len(cks) - 1))
                                for ii, cb in enumerate(cks):
                                    j = cb - b + 1
                                    ins = e.matmul(pO[:, 128:256], lhsT=ones_b[:], rhs=pt[:, j, :], start=(ii == 0), stop=(ii == len(cks) - 1))
                                return ins
                            P.op("pe", mm, reads=[ptb, B_vrm, B_onesb], writes=[pOb])
                            dst = ACC[:, :, cols(r, b)]
                            src = pO[:, 0:256].rearrange("p (a t) -> p a t", a=2)
                            if g == 0:
                                P.op("dve", (lambda e: e.tensor_copy(out=dst, in_=src)), reads=[pOb], writes=[B_acc])
                            else:
                                P.op("dve", (lambda e: e.tensor_tensor(out=dst, in0=dst, in1=src, op=ALU.add)), reads=[pOb, B_acc], writes=[B_acc])
                        prev = None
                        for (r, b) in allc:
                            cur_ = s1(r, b)
                            if prev is not None:
                                s2(prev)
                            prev = cur_
                        s2(prev)
                    P.op("dve", (lambda e: e.reciprocal(out=ACC[:, 1, :], in_=ACC[:, 1, :])), reads=[B_acc], writes=[B_acc])
                    P.op("dve", (lambda e: e.tensor_tensor(out=yout[:], in0=ACC[:, 0, :], in1=ACC[:, 1, :], op=ALU.mult)), reads=[B_acc], writes=[B_yout])
                    scr_store("pool", (lambda e, h=h: e.dma_start(out=YT[h * 128:(h + 1) * 128, :], in_=yout[:])), B_yout, "YT")
                P.barrier()
            cur[0] = st

        def mlstm():
            with contextlib.ExitStack() as pst:
                cur[0] = pst
                NCOL = NCH * HM
                MGc = sb("MGc", [128, NCH, 4 * HM], F32)
                B_mg = Buf("MGc")
                MGv = MG.rearrange("(c p) g -> p c g", p=128)
                for c0 in range(0, NCH, 8):
                    P.dma("sp", (lambda e, c0=c0: e.dma_start(out=MGc[:, c0:c0 + 8, :], in_=MGv[:, c0:c0 + 8, :])), B_mg, reads=[B_scr["MG"]], writes=[B_mg])
                gb = sb("gbias", [128, 4 * HM], F32)
                B_gb = Buf("gbias")
                P.dma("sp", lambda e: e.dma_start(out=gb[:], in_=w_in["mlstm_gate_bias"][0:1, :].partition_broadcast(128)), B_gb, writes=[B_gb])
                P.op("dve", lambda e: e.tensor_tensor(out=MGc[:], in0=MGc[:], in1=gb[:].unsqueeze(1).to_broadcast([128, NCH, 4 * HM]), op=ALU.add),
                     reads=[B_mg, B_gb], writes=[B_mg])
                ntri = sb("ntri", [128, 2, 128], F32)
                cmask = sb("cmask", [128, 2, 128], F32)
                nones = sb("nones", [128, 128], F32)
                B_tri = Buf("tri")
                P.op("pool", lambda e: e.memset(ntri[:], -1.0), writes=[B_tri])
                P.op("pool", lambda e: e.memset(cmask[:], 1.0), writes=[B_tri])
                P.op("pool", lambda e: e.memset(nones[:], -1.0), writes=[B_tri])
                for (tl) in (ntri, cmask):
                    P.op("pool", (lambda e, tl=tl: e.affine_select(out=tl[:, 0, :], in_=tl[:, 0, :], pattern=[[1, 128]], compare_op=ALU.is_ge,
                                                                   fill=0.0, base=0, channel_multiplier=-1)), reads=[B_tri], writes=[B_tri])
                    P.op("pool", (lambda e, tl=tl: e.affine_select(out=tl[:, 1, :], in_=tl[:, 1, :], pattern=[[-1, 128]], compare_op=ALU.is_ge,
                                                                   fill=0.0, base=0, channel_multiplier=1)), reads=[B_tri], writes=[B_tri])
                G = {}
                for nm in ("EB", "ET", "W", "W2"):
                    for di in range(2):
                        G[(nm, di)] = sb(f"g{nm}{di}", [128, NCOL], F32)
                B_G = Buf("gatevecs")
                spt = sb("spt", [128, NCH, HM], F32)
                B_sp = Buf("spt")
                for di, (ioff, foff) in enumerate(((0, HM), (2 * HM, 3 * HM))):
                    P.op("act", (lambda e, foff=foff: e.activation(out=spt[:], in_=MGc[:, :, foff:foff + HM], func=AF.Exp, scale=-1.0)),
                         reads=[B_mg, B_sp], writes=[B_sp])
                    P.op("act", (lambda e: e.activation(out=spt[:], in_=spt[:], func=AF.Ln, bias=1.0)), reads=[B_sp], writes=[B_sp])
                    spf = spt[:].rearrange("p c h -> p (c h)")
                    for g0 in range(0, NCOL, 512):
                        n = min(512, NCOL - g0)
                        ncg = n // HM
                        cg0 = g0 // HM
                        pb, pbb = bankrot.next()
                        pt_, ptb = bankrot.next()
                        P.op("pe", (lambda e, pb=pb, di=di, g0=g0, n=n, spf=spf: e.matmul(pb[:, 0:n], lhsT=ntri[:, di, :], rhs=spf[:, g0:g0 + n], start=True, stop=True)),
                             reads=[B_sp, B_tri], writes=[pbb])
                        P.op("pe", (lambda e, pt_=pt_, g0=g0, n=n, spf=spf: e.matmul(pt_[:, 0:n], lhsT=nones[:], rhs=spf[:, g0:g0 + n], start=True, stop=True)),
                             reads=[B_sp, B_tri], writes=[ptb])
                        EB, ET, W, W2 = G[("EB", di)], G[("ET", di)], G[("W", di)], G[("W2", di)]
                        P.op("act", (lambda e, EB=EB, pb=pb, g0=g0, n=n: e.activation(out=EB[:, g0:g0 + n], in_=pb[:, 0:n], func=AF.Exp)), reads=[pbb], writes=[B_G])
                        P.op("act", (lambda e, ET=ET, pt_=pt_, g0=g0, n=n: e.activation(out=ET[:, g0:g0 + n], in_=pt_[:, 0:n], func=AF.Exp)), reads=[ptb], writes=[B_G])
                        P.op("dve", (lambda e, W=W, pb=pb, g0=g0, n=n, cg0=cg0, ncg=ncg, ioff=ioff: e.tensor_tensor(
                            out=W[:, g0:g0 + n].rearrange("p (c h) -> p c h", h=HM), in0=MGc[:, cg0:cg0 + ncg, ioff:ioff + HM],
                            in1=pb[:, 0:n].rearrange("p (c h) -> p c h", h=HM), op=ALU.subtract)), reads=[pbb, B_mg, B_G], writes=[B_G])
                        P.op("act", (lambda e, W=W, g0=g0, n=n: e.activation(out=W[:, g0:g0 + n], in_=W[:, g0:g0 + n], func=AF.Exp)), reads=[B_G], writes=[B_G])
                        P.op("dve", (lambda e, W=W, W2=W2, ET=ET, g0=g0, n=n: e.tensor_tensor(out=W2[:, g0:g0 + n], in0=W[:, g0:g0 + n], in1=ET[:, g0:g0 + n], op=ALU.mult)),
                             reads=[B_G], writes=[B_G])

                if upto >= 4:
                  with contextlib.ExitStack() as p1st:
                    cur[0] = p1st
                    QT = Rot([(sb(f"mq{i}", [128, 2 * HM, 128], BF16), Buf(f"mq{i}")) for i in range(4)])
                    KT = Rot([(sb(f"mk{i}", [128, 2 * HM, 128], BF16), Buf(f"mk{i}")) for i in range(4)])
                    VA = []
                    for i in range(4):
                        t_ = sb(f"mv{i}", [128, HM, 257], BF16)
                        b_ = Buf(f"mv{i}")
                        P.op("pool", (lambda e, t_=t_: e.memset(t_[:], 1.0)), writes=[b_])
                        VA.append((t_, b_))
                    VA = Rot(VA)
                    Cst = sb("Cst", [128, 2 * HM, 2, 257], F32)
                    Cbf = sb("Cbf", [128, 2 * HM, 2, 257], BF16)
                    B_C = [Buf(f"C{i}") for i in range(2 * HM)]
                    B_Cb = [Buf(f"Cb{i}") for i in range(2 * HM)]
                    P.op("pool", lambda e: e.memset(Cst[:], 0.0), writes=B_C)
                    P.op("pool", lambda e: e.memset(Cbf[:], 0.0), writes=B_Cb)
                    PW = Rot([(sb(f"mpw{i}", [128, 128], BF16), Buf(f"mpw{i}")) for i in range(3)])
                    KW = Rot([(sb(f"mkw{i}", [128, 256], BF16), Buf(f"mkw{i}")) for i in range(3)])
                    HS = Rot([(sb(f"mhs{i}", [128, 257], F32), Buf(f"mhs{i}")) for i in range(4)])
                    for stp in range(NCH):
                        for di, ch in enumerate((stp, NCH - 1 - stp)):
                            qt, qb = QT.next()
                            kt, kb = KT.next()
                            va, vb = VA.next()
                            P.dma("sp", (lambda e, qt=qt, ch=ch: e.dma_start(out=qt[:], in_=QmC[ch])), qb, reads=[B_scr["QmC"]], writes=[qb])
                            P.dma("sp", (lambda e, kt=kt, ch=ch: e.dma_start(out=kt[:], in_=KmC[ch])), kb, reads=[B_scr["KmC"]], writes=[kb])
                            P.dma("sp", (lambda e, va=va, ch=ch: e.dma_start(out=va[:, :, 0:256], in_=Vm[ch * 128:(ch + 1) * 128, :].rearrange("p (h e) -> p h e", e=256))),
                                  vb, reads=[B_scr["Vm"]], writes=[vb])
                            W, W2, ET = G[("W", di)], G[("W2", di)], G[("ET", di)]
                            for hd in range(HM):
                                col = ch * HM + hd
                                sidx = di * HM + hd
                                pS, pSb = bankrot.next()

                                def mmS(e, pS=pS, kt=kt, qt=qt, hd=hd):
                                    ins = None
                                    for dcn in range(2):
                                        ins = e.matmul(pS[:, 0:128], lhsT=kt[:, 2 * hd + dcn, :], rhs=qt[:, 2 * hd + dcn, :], start=(dcn == 0), stop=(dcn == 1))
                                    return ins
                                P.op("pe", mmS, reads=[kb, qb], writes=[pSb])
                                pw, pwb = PW.next()
                                P.op("dve", (lambda e, pw=pw, pS=pS, W=W, col=col, di=di: e.scalar_tensor_tensor(out=pw[:], in0=pS[:, 0:128], scalar=W[:, col:col + 1],
                                                                                                                  in1=cmask[:, di, :], op0=ALU.mult, op1=ALU.mult)),
                                     reads=[pSb, B_G, B_tri], writes=[pwb])
                                pK, pKb = bankrot.next()
                                pKv = pK[:].bitcast(BF16)

                                def trK(e, pKv=pKv, kt=kt, hd=hd):
                                    ins = None
                                    for dcn in range(2):
                                        ins = e.transpose(pKv[:, dcn * 128:(dcn + 1) * 128], kt[:, 2 * hd + dcn, :], ident_b[:])
                                    return ins
                                P.op("pe", trK, reads=[kb, B_identb], writes=[pKb])
                                kw, kwb = KW.next()
                                P.op("act", (lambda e, kw=kw, pKv=pKv, W2=W2, col=col: e.activation(out=kw[:], in_=pKv[:, 0:256], func=AF.Copy, scale=W2[:, col:col + 1])),
                                     reads=[pKb, B_G], writes=[kwb])
                                pA, pAb = bankrot.next()

                                def mmA(e, pA=pA, pw=pw, va=va, qt=qt, hd=hd, sidx=sidx):
                                    e.matmul(pA[:, 0:257], lhsT=pw[:], rhs=va[:, hd, :], start=True, stop=False)
                                    ins = None
                                    for dcn in range(2):
                                        ins = e.matmul(pA[:, 0:257], lhsT=qt[:, 2 * hd + dcn, :], rhs=Cbf[:, sidx, dcn, :], start=False, stop=(dcn == 1))
                                    return ins
                                P.op("pe", mmA, reads=[pwb, vb, qb, B_Cb[sidx]], writes=[pAb])
                                hs, hsb = HS.next()
                                copy_evac(hs[:], pA[:, 0:257], [pAb], [hsb])
                                scr_store("pool", (lambda e, hs=hs, di=di, ch=ch, hd=hd: e.dma_start(out=HRAW[di, ch * 128:(ch + 1) * 128, hd, :], in_=hs[:])), hsb, "HRAW")
                                pU = [bankrot.next(), bankrot.next()]

                                def mmU(e, pU=pU, kw=kw, va=va, hd=hd):
                                    ins = None
                                    for dcn in range(2):
                                        ins = e.matmul(pU[dcn][0][:, 0:257], lhsT=kw[:, dcn * 128:(dcn + 1) * 128], rhs=va[:, hd, :], start=True, stop=True)
                                    return ins
                                P.op("pe", mmU, reads=[kwb, vb], writes=[pU[0][1], pU[1][1]])
                                for dcn in range(2):
                                    P.op("dve", (lambda e, dcn=dcn, pu=pU[dcn][0], sidx=sidx, ET=ET, col=col: e.scalar_tensor_tensor(
                                        out=Cst[:, sidx, dcn, :], in0=Cst[:, sidx, dcn, :], scalar=ET[:, col:col + 1], in1=pu[:, 0:257], op0=ALU.mult, op1=ALU.add)),
                                        reads=[pU[dcn][1], B_G, B_C[sidx]], writes=[B_C[sidx]])
                                P.op("act", (lambda e, sidx=sidx: e.copy(out=Cbf[:, sidx, :, :], in_=Cst[:, sidx, :, :])), reads=[B_C[sidx]], writes=[B_Cb[sidx]])
                    P.barrier()

                cur[0] = pst
                if upto >= 5:
                  with contextlib.ExitStack() as p2st:
                    cur[0] = p2st
                    hnw = sb("hnw", [128, MW], F32)
                    B_hnw = Buf("hnw")
                    P.dma("sp", lambda e: e.dma_start(out=hnw[:], in_=w_in["mlstm_head_norm"][0:1, :].partition_broadcast(128)), B_hnw, writes=[B_hnw])
                    RF = Rot([(sb(f"rf{i}", [128, HM, 257], F32), Buf(f"rf{i}")) for i in range(2)])
                    RB = Rot([(sb(f"rb{i}", [128, HM, 257], F32), Buf(f"rb{i}")) for i in range(2)])
                    OM = Rot([(sb(f"om{i}", [128, MW], F32), Buf(f"om{i}")) for i in range(2)])
                    HH = Rot([(sb(f"hh{i}", [128, HM, 256], F32), Buf(f"hh{i}")) for i in range(2)])
                    TT_ = Rot([(sb(f"ht{i}", [128, HM, 256], F32), Buf(f"ht{i}")) for i in range(2)])
                    YB = Rot([(sb(f"yb{i}", [128, MW], BF16), Buf(f"yb{i}")) for i in range(2)])
                    YS = Rot([(sb(f"ys{i}", [128, MW // 128, 128], BF16), Buf(f"ys{i}")) for i in range(2)])
                    SM = Rot([(sb(f"sm{i}", [128, 8, HM], F32), Buf(f"sm{i}")) for i in range(2)])
                    def do_chunk(ch):
                        rows = slice(ch * 128, (ch + 1) * 128)
                        rf, rfb = RF.next()
                        rb, rbb = RB.next()
                        om, omb = OM.next()
                        P.dma("sp", (lambda e, rf=rf, rows=rows: e.dma_start(out=rf[:], in_=HRAW[0, rows, :, :])), rfb, reads=[B_scr["HRAW"]], writes=[rfb])
                        P.dma("sp", (lambda e, rb=rb, rows=rows: e.dma_start(out=rb[:], in_=HRAW[1, rows, :, :])), rbb, reads=[B_scr["HRAW"]], writes=[rbb])
                        P.dma("sp", (lambda e, om=om, rows=rows: e.dma_start(out=om[:], in_=Om[rows, :])), omb, reads=[B_scr["Om"]], writes=[omb])
                        sm, smb = SM.next()
                        cs = slice(ch * HM, (ch + 1) * HM)
                        for di, (rr_, rrb) in enumerate(((rf, rfb), (rb, rbb))):
                            EB = G[("EB", di)]
                            den, cf = sm[:, 2 * di, :], sm[:, 2 * di + 1, :]
                            P.op("dve", (lambda e, den=den, rr_=rr_, EB=EB: e.tensor_tensor(out=den, in0=rr_[:, :, 256], in1=EB[:, cs], op=ALU.mult)), reads=[rrb, B_G, smb], writes=[smb])
                            neg = sm[:, 7, :]
                            P.op("dve", (lambda e, den=den, neg=neg: e.tensor_scalar(out=neg, in0=den, scalar1=-1.0, scalar2=None, op0=ALU.mult)), reads=[smb], writes=[smb])
                            P.op("dve", (lambda e, den=den, neg=neg: e.scalar_tensor_tensor(out=den, in0=neg, scalar=1.0, in1=den, op0=ALU.max, op1=ALU.max)), reads=[smb], writes=[smb])
                            P.op("dve", (lambda e, den=den: e.reciprocal(out=den, in_=den)), reads=[smb], writes=[smb])
                            P.op("dve", (lambda e, den=den, cf=cf, EB=EB: e.tensor_tensor(out=cf, in0=den, in1=EB[:, cs], op=ALU.mult)), reads=[smb, B_G], writes=[smb])
                        hh, hhb = HH.next()
                        tt, ttb = TT_.next()
                        bc = lambda ap: ap.unsqueeze(2).to_broadcast([128, HM, 256])
                        P.op("dve", (lambda e: e.tensor_tensor(out=hh[:], in0=rf[:, :, 0:256], in1=bc(sm[:, 1, :]), op=ALU.mult)), reads=[rfb, smb], writes=[hhb])
                        P.op("pool", (lambda e: e.tensor_tensor(out=tt[:], in0=rb[:, :, 0:256], in1=bc(sm[:, 3, :]), op=ALU.mult)), reads=[rbb, smb], writes=[ttb])
                        P.op("dve", (lambda e: e.tensor_tensor(out=hh[:], in0=hh[:], in1=tt[:], op=ALU.add)), reads=[hhb, ttb], writes=[hhb])
                        P.op("dve", (lambda e: e.tensor_reduce(out=sm[:, 4, :], in_=hh[:], axis=AX.X, op=ALU.add)), reads=[hhb, smb], writes=[smb])
                        P.op("pool", (lambda e: e.tensor_tensor(out=tt[:], in0=hh[:], in1=hh[:], op=ALU.mult)), reads=[hhb, ttb], writes=[ttb])
                        P.op("dve", (lambda e: e.tensor_reduce(out=sm[:, 5, :], in_=tt[:], axis=AX.X, op=ALU.add)), reads=[ttb, smb], writes=[smb])
                        P.op("dve", (lambda e: e.tensor_scalar(out=sm[:, 4, :], in0=sm[:, 4, :], scalar1=1.0 / 256, scalar2=None, op0=ALU.mult)), reads=[smb], writes=[smb])
                        P.op("dve", (lambda e: e.tensor_tensor(out=sm[:, 6, :], in0=sm[:, 4, :], in1=sm[:, 4, :], op=ALU.mult)), reads=[smb], writes=[smb])
                        P.op("dve", (lambda e: e.scalar_tensor_tensor(out=sm[:, 5, :], in0=sm[:, 5, :], scalar=1.0 / 256, in1=sm[:, 6, :], op0=ALU.mult, op1=ALU.subtract)), reads=[smb], writes=[smb])
                        P.op("dve", (lambda e: e.tensor_scalar(out=sm[:, 5, :], in0=sm[:, 5, :], scalar1=NORM_EPS, scalar2=None, op0=ALU.add)), reads=[smb], writes=[smb])
                        P.op("act", (lambda e: e.activation(out=sm[:, 5, :], in_=sm[:, 5, :], func=AF.Sqrt)), reads=[smb], writes=[smb])
                        P.op("dve", (lambda e: e.reciprocal(out=sm[:, 5, :], in_=sm[:, 5, :])), reads=[smb], writes=[smb])
                        P.op("dve", (lambda e: e.tensor_tensor(out=hh[:], in0=hh[:], in1=bc(sm[:, 4, :]), op=ALU.subtract)), reads=[hhb, smb], writes=[hhb])
                        P.op("dve", (lambda e: e.tensor_tensor(out=hh[:], in0=hh[:], in1=bc(sm[:, 5, :]), op=ALU.mult)), reads=[hhb, smb], writes=[hhb])
                        hflat = hh[:].rearrange("p h e -> p (h e)")
                        P.op("pool", (lambda e: e.tensor_tensor(out=hflat, in0=hflat, in1=hnw[:], op=ALU.mult)), reads=[hhb, B_hnw], writes=[hhb])
                        yb, ybb = YB.next()
                        P.op("dve", (lambda e: e.tensor_tensor(out=yb[:], in0=hflat, in1=om[:], op=ALU.mult)), reads=[hhb, omb], writes=[ybb])
                        ys, ysb = YS.next()
                        for k4 in range(0, MW // 128, 4):
                            pb, pbb = bankrot.next()
                            pbv = pb[:].bitcast(BF16)

                            def tr(e, pbv=pbv, yb=yb, k4=k4):
                                ins = None
                                for q in range(4):
                                    ins = e.transpose(pbv[:, q * 128:(q + 1) * 128], yb[:, (k4 + q) * 128:(k4 + q + 1) * 128], ident_b[:])
                                return ins
                            P.op("pe", tr, reads=[ybb, B_identb], writes=[pbb])
                            copy_evac(ys[:, k4:k4 + 4, :], pbv[:, 0:512].rearrange("p (q t) -> p q t", q=4), [pbb], [ysb])
                        scr_store("pool", (lambda e, ys=ys, ch=ch: e.dma_start(out=YT[AO:AO + MW, ch * 128:(ch + 1) * 128].rearrange("(k p) t -> p k t", p=128), in_=ys[:])), ysb, "YT")
                    for ch in range(NCH):
                        do_chunk(ch)
                    P.barrier()
            cur[0] = st

        def phaseC():
            with contextlib.ExitStack() as pst:
                cur[0] = pst
                K = make_core(5)
                xT, B_xT, hT, B_hT, fpool, wq = K.xT, K.B_xT, K.hT, K.B_hT, K.fpool, K.wq
                YTt = sb("YTt", [128, KY, 512], BF16)
                B_yt = Buf("YTt")
                o32 = Rot([(sb(f"o32_{i}", [128, 4, 512], F32), Buf(f"o32_{i}")) for i in range(2)])
                X1v = X1T.rearrange("(k p) t -> p k t", p=128)
                YTv = YT.rearrange("(k p) t -> p k t", p=128)
                B_xld = Buf("xld")
                B_out = Buf("outst")

                def plan_tile():
                    for dc in range(DC):
                        wq.add(cWb[dc, :, :], KY * 128)
                    for dc in range(DC):
                        wq.add(cWout[dc, :, :], DC * 128)
                    K.plan_ffn(2)

                def tile(i):
                    cols = slice(i * 512, (i + 1) * 512)
                    for k0 in range(0, DC, 8):
                        nk = min(8, DC - k0)
                        P.dma("sp", (lambda e, k0=k0, nk=nk: e.dma_start(out=xT[:, k0:k0 + nk, :], in_=X1v[:, k0:k0 + nk, cols])),
                              B_xld, reads=[B_scr["X1T"]], writes=B_xT[k0:k0 + nk])
                    P.seal(B_xld, B_xT, "w")
                    for k0 in range(0, KY, 8):
                        nk = min(8, KY - k0)
                        P.dma("sp", (lambda e, k0=k0, nk=nk: e.dma_start(out=YTt[:, k0:k0 + nk, :], in_=YTv[:, k0:k0 + nk, cols])),
                              B_yt, reads=[B_scr["YT"]], writes=[B_yt])
                    for dc in range(DC):
                        wt, wb = wq.get()
                        pA, pAb = bankrot.next()
                        pM, pMb = bankrot.next()

                        def mm(e, wt=wt, pA=pA, pM=pM):
                            ins = None
                            for k in range(KA):
                                ins = e.matmul(pA[:], lhsT=wt[:, k * 128:(k + 1) * 128], rhs=YTt[:, k, :], start=(k == 0), stop=(k == KA - 1))
                            for k in range(KA, KY):
                                ins = e.matmul(pM[:], lhsT=wt[:, k * 128:(k + 1) * 128], rhs=YTt[:, k, :], start=(k == KA), stop=(k == KY - 1))
                            return ins
                        P.op("pe", mm, reads=[wb, B_yt], writes=[pAb, pMb])
                        ga, gab = fpool.next()
                        gm, gmb = fpool.next()
                        P.dma("sp", (lambda e, ga=ga, dc=dc: e.dma_start(out=ga[:], in_=GT[dc * 128:(dc + 1) * 128, cols])), gab, reads=[B_scr["GT"]], writes=[gab])
                        P.dma("sp", (lambda e, gm=gm, dc=dc: e.dma_start(out=gm[:], in_=GT[D + dc * 128:D + (dc + 1) * 128, cols])), gmb, reads=[B_scr["GT"]], writes=[gmb])
                        P.op("dve", (lambda e, ga=ga, pA=pA: e.tensor_tensor(out=ga[:], in0=ga[:], in1=pA[:], op=ALU.mult)), reads=[gab, pAb], writes=[gab])
                        P.op("dve", (lambda e, gm=gm, pM=pM: e.tensor_tensor(out=gm[:], in0=gm[:], in1=pM[:], op=ALU.mult)), reads=[gmb, pMb], writes=[gmb])
                        P.op("pool", (lambda e, ga=ga, gm=gm, dc=dc: e.tensor_tensor(out=hT[:, dc, :], in0=ga[:], in1=gm[:], op=ALU.add)), reads=[gab, gmb], writes=[B_hT[dc]])
                    for dc in range(DC):
                        wt, wb = wq.get()
                        pb, pbb = bankrot.next()

                        def mm(e, wt=wt, pb=pb):
                            ins = None
                            for k in range(DC):
                                ins = e.matmul(pb[:], lhsT=wt[:, k * 128:(k + 1) * 128], rhs=hT[:, k, :], start=(k == 0), stop=(k == DC - 1))
                            return ins
                        P.op("pe", mm, reads=[wb] + B_hT, writes=[pbb])
                        P.op("dve", (lambda e, dc=dc, pb=pb: e.tensor_tensor(out=xT[:, dc, :], in0=xT[:, dc, :], in1=pb[:], op=ALU.add)),
                             reads=[pbb, B_xT[dc]], writes=[B_xT[dc]])
                    K.rmsnorm_T(2)
                    K.ffn(2)
                    K.rms_stats()
                    for dc0 in range(0, DC, 4):
                        nd = min(4, DC - dc0)
                        ot, otb = o32.next()
                        for q in range(nd):
                            dc = dc0 + q
                            P.op("dve", (lambda e, ot=ot, q=q, dc=dc: e.scalar_tensor_tensor(out=ot[:, q, :], in0=xT[:, dc, :], scalar=gains[:, 3, dc:dc + 1],
                                                                                               in1=K.rstd_bc[:], op0=ALU.mult, op1=ALU.mult)),
                                 reads=[B_xT[dc], B_gains, K.B_rstd], writes=[otb])
                        for tcn in range(4):
                            pb, pbb = bankrot.next()

                            def tr(e, pb=pb, ot=ot, tcn=tcn, nd=nd):
                                ins = None
                                for q in range(nd):
                                    ins = e.transpose(pb[:, q * 128:(q + 1) * 128], ot[:, q, tcn * 128:(tcn + 1) * 128], ident_f[:])
                                return ins
                            P.op("pe", tr, reads=[otb, B_ident], writes=[pbb])
                            os_, osb = fpool.next()
                            copy_evac(os_[:, 0:nd * 128], pb[:, 0:nd * 128], [pbb], [osb])
                            r0 = i * 512 + tcn * 128
                            scr_store("pool", (lambda e, os_=os_, r0=r0, dc0=dc0, nd=nd: e.dma_start(out=out[r0:r0 + 128, dc0 * 128:(dc0 + nd) * 128], in_=os_[:, 0:nd * 128])),
                                      osb, "out")

                for i in range(NT):
                    plan_tile()
                for i in range(NT):
                    tile(i)
                P.barrier()
            cur[0] = st

        if "nopre" not in dbg_names:
            prepass()
        if "pa0" not in dbg_names:
            phaseA()
        if upto >= 2:
            conv_pass()
        if upto >= 3:
            attention()
        if upto >= 4:
            mlstm()
        if upto >= 6:
            phaseC()
        P.finish(list(B_scr.values()))
        with nc.Block() as block:
            P.replay(block)
    return nc, dbg


_CACHE = {}


def kernel(**inputs):
    cfg = Cfg()
    if "nc" not in _CACHE:
        _CACHE["nc"] = build(cfg)[0]
    nc = _CACHE["nc"]
    x = np.asarray(inputs["x"], dtype=np.float32)
    B = x.shape[0]
    shared = {}
    for k, v in inputs.items():
        if k == "x":
            continue
        a = np.asarray(v, dtype=np.float32)
        if a.ndim == 3:
            a = a[0]
        elif a.ndim == 2:
            a = a.reshape(1, -1)
        else:
            a = a.reshape(1, -1)
        shared[k] = np.ascontiguousarray(a)
    in_maps = []
    for b in range(B):
        m = dict(shared)
        m["x"] = np.ascontiguousarray(x[b])
        in_maps.append(m)
    res = run_bass_kernel_spmd(nc, in_maps, core_ids=list(range(B)))
    return np.stack([np.asarray(res.results[b]["out"], dtype=np.float32) for b in range(B)], axis=0)
```

```python
import numpy as np
import ml_dtypes
import concourse.bass as bass
import concourse.mybir as mybir
from concourse.bass_utils import run_bass_kernel_spmd

F32 = mybir.dt.float32
BF16 = mybir.dt.bfloat16
I32 = mybir.dt.int32
AF = mybir.ActivationFunctionType
ALU = mybir.AluOpType
AX = mybir.AxisListType

NORM_EPS = 1e-6
ROPE_THETA = 500000.0


class Cfg:
    def __init__(self, D=4096, F=11008, T=8192, HA=8, HM=8, B=2):
        self.D, self.F, self.T, self.HA, self.HM, self.B = D, F, T, HA, HM, B
        self.TT = 512
        self.NT = T // 512
        self.DC = D // 128
        self.FC = F // 128
        assert F % 128 == 0 and D % 128 == 0 and T % 1024 == 0
        self.AW = 3 * HA * 128
        self.AO = HA * 128
        self.MW = HM * 256
        o = 0
        self.o_qa = o; o += self.AW
        self.o_ka = o; o += self.AW
        self.o_va = o; o += self.AW
        self.o_qm = o; o += self.MW
        self.o_km = o; o += self.MW
        self.o_vm = o; o += self.MW
        self.o_om = o; o += self.MW
        self.o_mg = o; o += 4 * HM
        self.o_ga = o; o += D
        self.o_gm = o; o += D
        self.IN = o
        ng = -(-self.FC // 11)
        base, rem = divmod(self.FC, ng)
        self.fgroups = []
        s = 0
        for g in range(ng):
            n = base + (1 if g < rem else 0)
            self.fgroups.append((s, n))
            s += n
        self.GMAX = max(n for _, n in self.fgroups)


class Buf:
    __slots__ = ("name", "w", "r", "dsem")

    def __init__(self, name):
        self.name = name
        self.w = {}
        self.r = {}
        self.dsem = None


class Eng:
    def __init__(self, name, sem, idx):
        self.name, self.sem, self.idx = name, sem, idx
        self.cnt = 0
        self.seen = {}
        self.q = []


class Prog:
    def __init__(self, nc, sem_list):
        self.nc = nc
        self.free_sems = list(sem_list)
        self.sem_idx = {}
        self.engs = {}
        for n in ("pe", "act", "dve", "pool", "sp"):
            s = self.free_sems.pop()
            self.sem_idx[id(s)] = len(self.sem_idx)
            self.engs[n] = Eng(n, s, self.sem_idx[id(s)])
        self.dcount = {}
        self.dsems = {}
        self.final = []

    def _need(self, eng, reads, writes):
        need = {}

        def add(d):
            for k, (sem, v) in d.items():
                if k not in need or need[k][1] < v:
                    need[k] = (sem, v)
        for b in reads:
            add(b.w)
        for b in writes:
            add(b.w)
            add(b.r)
        out = []
        for k, (sem, v) in need.items():
            if eng.name == "pe" and k == eng.idx:
                continue
            if eng.seen.get(k, 0) >= v:
                continue
            eng.seen[k] = v
            out.append((sem, v))
        return out

    def _mark(self, tok, key, reads, writes):
        for b in reads:
            if key not in b.r or b.r[key][1] < tok[1]:
                b.r[key] = tok
        for b in writes:
            b.w = {key: tok}
            b.r = {}

    def op(self, en, fn, reads=(), writes=()):
        eng = self.engs[en]
        waits = self._need(eng, reads, writes)
        eng.cnt += 1
        tok = (eng.sem, eng.cnt)
        eng.q.append((waits, fn, eng.sem, 1))
        self._mark(tok, eng.idx, reads, writes)
        return tok

    def dma(self, en, fn, sbuf, reads=(), writes=()):
        eng = self.engs[en]
        if sbuf.dsem is None:
            s = self.free_sems.pop()
            self.sem_idx[id(s)] = len(self.sem_idx)
            sbuf.dsem = s
            self.dcount[id(s)] = 0
            self.dsems[id(s)] = s
        sem = sbuf.dsem
        waits = self._need(eng, reads, writes)
        self.dcount[id(sem)] += 16
        tok = (sem, self.dcount[id(sem)])
        eng.q.append((waits, fn, sem, 16))
        self._mark(tok, self.sem_idx[id(sem)], reads, writes)
        return tok

    def seal(self, owner, bufs, kind="r"):
        sem = owner.dsem
        key = self.sem_idx[id(sem)]
        tok = (sem, self.dcount[id(sem)])
        for b in bufs:
            if kind == "r":
                b.r[key] = tok
            else:
                b.w[key] = tok

    def barrier(self):
        allv = [(e.sem, e.cnt, e.idx) for e in self.engs.values() if e.cnt > 0]
        allv += [(s, self.dcount[i], self.sem_idx[i]) for i, s in self.dsems.items() if self.dcount[i] > 0]
        for eng in self.engs.values():
            waits = []
            for (sem, v, k) in allv:
                if eng.name == "pe" and k == eng.idx:
                    continue
                if eng.seen.get(k, 0) >= v:
                    continue
                eng.seen[k] = v
                waits.append((sem, v))
            eng.q.append((waits, None, None, 0))

    def finish(self, bufs):
        eng = self.engs["sp"]
        waits = self._need(eng, bufs, ())
        eng.q.append((waits, None, None, 0))

    def replay(self, block):
        nc = self.nc
        P = self

        def run(e, q):
            for waits, fn, sem, inc in q:
                for (s, v) in waits:
                    e.wait_ge(s, v)
                if fn is None:
                    continue
                ins = fn(e)
                ins.then_inc(sem, inc)

        @block.tensor
        def _(e):
            run(e, P.engs["pe"].q)

        @block.scalar
        def _(e):
            run(e, P.engs["act"].q)

        @block.vector
        def _(e):
            run(e, P.engs["dve"].q)

        @block.gpsimd
        def _(e):
            run(e, P.engs["pool"].q)

        @block.sync
        def _(e):
            run(e, P.engs["sp"].q)


class Rot:
    def __init__(self, items):
        self.items = items
        self.i = 0

    def next(self):
        it = self.items[self.i % len(self.items)]
        self.i += 1
        return it


def build(cfg, debug=None):
    import contextlib
    c = cfg
    D, F, T, DC, FC, NT = c.D, c.F, c.T, c.DC, c.FC, c.NT
    HA, HM, AW, MW, AO = c.HA, c.HM, c.AW, c.MW, c.AO
    NCH = T // 128
    KA = AO // 128
    KY = (AO + MW) // 128
    NB = D // 256
    GMAX = c.GMAX
    NG = len(c.fgroups)
    SLOT = 4096
    KCAST = 8
    nc = bass.Bass("TRN2", target_bir_lowering=False)
    dbg = {}
    dbg_names = set(debug.split(",")) if isinstance(debug, str) else set()
    order = ["A", "conv", "attn", "m1", "m2", "C"]
    upto = len(order)
    for i_, nm_ in enumerate(order):
        if ("upto_" + nm_) in dbg_names:
            upto = i_ + 1

    def din(name, shape, dt=F32):
        return nc.dram_tensor(name, list(shape), dt, kind="ExternalInput").ap()

    def dscr(name, shape, dt):
        if name in dbg_names:
            t_ = nc.dram_tensor(name, list(shape), dt, kind="ExternalOutput").ap()
            dbg[name] = t_
            return t_
        return nc.dram_tensor(name, list(shape), dt, kind="Internal").ap()

    x_in = din("x", [T, D])
    w_in = {}
    for nm, shp in [("ffn1_norm", [1, D]), ("ffn1_w_gate", [D, F]), ("ffn1_w_up", [D, F]), ("ffn1_w_down", [F, D]),
                    ("mix_norm", [1, D]), ("w_in", [D, c.IN]), ("mlstm_conv_w", [5, 2 * MW]),
                    ("mlstm_conv_b", [1, 2 * MW]), ("mlstm_gate_bias", [1, 4 * HM]),
                    ("mlstm_head_norm", [1, MW]), ("w_branch_attn", [AO, D]), ("w_branch_mlstm", [MW, D]),
                    ("w_out", [D, D]), ("ffn2_norm", [1, D]), ("ffn2_w_gate", [D, F]), ("ffn2_w_up", [D, F]),
                    ("ffn2_w_down", [F, D]), ("final_norm", [1, D])]:
        w_in[nm] = din(nm, shp)
    out = nc.dram_tensor("out", [T, D], F32, kind="ExternalOutput").ap()

    cW = {}
    for l in (1, 2):
        cW[f"g{l}"] = dscr(f"cWg{l}", [FC, 128, DC * 128], BF16)
        cW[f"u{l}"] = dscr(f"cWu{l}", [FC, 128, DC * 128], BF16)
        cW[f"d{l}"] = dscr(f"cWd{l}", [NG * NB, 128, GMAX * 256], BF16)
    s_fams = [("qa", c.o_qa, 3 * HA), ("ka", c.o_ka, 3 * HA), ("va", c.o_va, 3 * HA), ("qm", c.o_qm, 2 * HM),
              ("km", c.o_km, 2 * HM), ("ga", c.o_ga, DC), ("gm", c.o_gm, DC)]
    s_blocks = [(fam, off + 128 * j, j) for (fam, off, n) in s_fams for j in range(n)]
    cWin_s = dscr("cWin_s", [len(s_blocks), 128, DC * 128], BF16)
    KH = min(DC, 8)
    m_blocks = []
    for fam, off, width in (("vm", c.o_vm, MW), ("om", c.o_om, MW), ("mg", c.o_mg, 4 * HM)):
        for j, c0 in enumerate(range(0, width, 512)):
            ncols = min(512, width - c0)
            for k0 in range(0, DC, KH):
                m_blocks.append((fam, off + c0, ncols, k0, min(KH, DC - k0), j))
    cWin_m = dscr("cWin_m", [len(m_blocks), 128, SLOT], BF16)
    cWb = dscr("cWb", [DC, 128, KY * 128], BF16)
    cWout = dscr("cWout", [DC, 128, DC * 128], BF16)
    cWbuf = Buf("cW")

    X1T = dscr("X1T", [D, T], F32)
    QaT = dscr("QaT", [AW, T], BF16)
    KaT = dscr("KaT", [AW, T], BF16)
    VaT = dscr("VaT", [AW, T], BF16)
    QKm = dscr("QKm", [2 * MW, T], F32)
    GT = dscr("GT", [2 * D, T], F32)
    Vm = dscr("Vm", [T, MW], BF16)
    Om = dscr("Om", [T, MW], F32)
    MG = dscr("MG", [T, 4 * HM], F32)
    QmC = dscr("QmC", [NCH, 128, 2 * HM, 128], BF16)
    KmC = dscr("KmC", [NCH, 128, 2 * HM, 128], BF16)
    HRAW = dscr("HRAW", [2, T, HM, 257], F32)
    YT = dscr("YT", [AO + MW, T], BF16)
    B_scr = {k: Buf(k) for k in ("X1T", "QaT", "KaT", "VaT", "QKm", "GT", "Vm", "Om", "MG", "YT", "QmC", "KmC", "HRAW", "out")}

    st = contextlib.ExitStack()
    with st:
        sems = [st.enter_context(nc.semaphore(f"s{i}")) for i in range(96)]
        P = Prog(nc, sems)
        cur = [st]

        sbn = [0]

        def sb(name, shape, dt):
            sbn[0] += 1
            return cur[0].enter_context(nc.sbuf_tensor(f"{name}_{sbn[0]}", list(shape), dt))

        def scr_store(q, fn, sbuf_b, scr_key, owner=None):
            ow = owner if owner is not None else sbuf_b
            if owner is None:
                P.dma(q, fn, sbuf_b, reads=[sbuf_b], writes=[])
            else:
                P.dma(q, fn, owner, reads=[sbuf_b], writes=[])
                P.seal(owner, [sbuf_b], "r")
            B_scr[scr_key].w[P.sem_idx[id(ow.dsem)]] = (ow.dsem, P.dcount[id(ow.dsem)])

        ones_f = sb("ones_f", [128, 128], F32)
        B_ones = Buf("ones_f")
        P.op("pool", lambda e: e.memset(ones_f[:], 1.0), writes=[B_ones])
        ones_b = sb("ones_b", [128, 128], BF16)
        B_onesb = Buf("ones_b")
        P.op("pool", lambda e: e.memset(ones_b[:], 1.0), writes=[B_onesb])
        ident_f = sb("ident_f", [128, 128], F32)
        B_ident = Buf("ident_f")
        P.op("pool", lambda e: e.memset(ident_f[:], 1.0), writes=[B_ident])
        P.op("pool", lambda e: e.affine_select(out=ident_f[:], in_=ident_f[:], pattern=[[-1, 128]], compare_op=ALU.is_equal,
                                               fill=0.0, base=0, channel_multiplier=1), reads=[B_ident], writes=[B_ident])
        ident_b = sb("ident_b", [128, 128], BF16)
        B_identb = Buf("ident_b")
        P.op("pool", lambda e: e.tensor_copy(out=ident_b[:], in_=ident_f[:]), reads=[B_ident], writes=[B_identb])
        gains = sb("gains", [128, 4, DC], F32)
        B_gains = Buf("gains")
        cvt = Rot([(sb(f"cvt{i}", [128, 128], F32), Buf(f"cvt{i}")) for i in range(2)])

        def load_colvec(dst, dst_buf, src_row, n):
            t_, tb = cvt.next()
            P.dma("sp", (lambda e: e.dma_start(out=t_[0:n, :], in_=src_row.rearrange("o (k p) -> (o k) p", p=128))), tb, writes=[tb])
            pb, pbb = bankrot.next()
            P.op("pe", (lambda e: e.transpose(pb[:, 0:n], t_[0:n, :], ident_f[0:n, 0:n])), reads=[tb, B_ident], writes=[pbb])
            P.op("dve", (lambda e: e.tensor_copy(out=dst, in_=pb[:, 0:n])), reads=[pbb, dst_buf], writes=[dst_buf])

        banks = []
        for i in range(8):
            t_ = st.enter_context(nc.psum_tensor(f"bank{i}", [128, 512], F32))
            banks.append((t_, Buf(f"bank{i}")))
        bankrot = Rot(banks)
        for i, nm in enumerate(["ffn1_norm", "mix_norm", "ffn2_norm", "final_norm"]):
            load_colvec(gains[:, i, :], B_gains, w_in[nm], DC)

        evac_rr = [0]

        def copy_evac(dst, src, reads, writes):
            en = ("act", "dve")[evac_rr[0] % 2]
            evac_rr[0] += 1
            if en == "act":
                P.op("act", (lambda e: e.copy(out=dst, in_=src)), reads=reads, writes=writes)
            else:
                P.op("dve", (lambda e: e.tensor_copy(out=dst, in_=src)), reads=reads, writes=writes)

        def make_caster():
                stg = Rot([(sb(f"stg{i}", [128, SLOT], F32), Buf(f"stg{i}")) for i in range(3)])
                bfs = Rot([(sb(f"bfs{i}", [128, SLOT], BF16), Buf(f"bfs{i}")) for i in range(3)])
                rr = [0]

                def cast_block(src3, nk, ncols, dst2):
                    ne = nk * ncols
                    assert ne <= SLOT
                    s_t, s_b = stg.next()
                    b_t, b_b = bfs.next()
                    P.dma("sp", (lambda e: e.dma_start(out=s_t[:, 0:ne].rearrange("p (k c) -> p k c", k=nk), in_=src3)),
                          s_b, writes=[s_b])
                    en = ("act", "dve", "pool")[rr[0] % 3]
                    rr[0] += 1
                    if en == "act":
                        P.op("act", (lambda e: e.copy(out=b_t[:, 0:ne], in_=s_t[:, 0:ne])), reads=[s_b], writes=[b_b])
                    else:
                        P.op(en, (lambda e: e.tensor_copy(out=b_t[:, 0:ne], in_=s_t[:, 0:ne])), reads=[s_b], writes=[b_b])
                    P.dma("pool", (lambda e: e.dma_start(out=dst2, in_=b_t[:, 0:ne])), b_b, reads=[b_b], writes=[])
                    cWbuf.w[P.sem_idx[id(b_b.dsem)]] = (b_b.dsem, P.dcount[id(b_b.dsem)])
                return cast_block

        def cast_jobs(part):
                jobs = []

                def J(*a):
                    jobs.append(lambda cb, a=a: cb(*a))
                cast_block = J
                for l in ((1,) if part == 1 else (2,)):
                    wg = w_in[f"ffn{l}_w_gate"].rearrange("(k p) n -> p k n", p=128)
                    wu = w_in[f"ffn{l}_w_up"].rearrange("(k p) n -> p k n", p=128)
                    wd = w_in[f"ffn{l}_w_down"].rearrange("(k p) n -> p k n", p=128)
                    for fc in range(FC):
                        for (w3, key) in ((wg, f"g{l}"), (wu, f"u{l}")):
                            for k0 in range(0, DC, KCAST):
                                nk = min(KCAST, DC - k0)
                                cast_block(w3[:, k0:k0 + nk, fc * 128:(fc + 1) * 128], nk, 128,
                                           cW[key][fc, :, k0 * 128:(k0 + nk) * 128])
                    for g, (f0, n) in enumerate(c.fgroups):
                        for b in range(NB):
                            cast_block(wd[:, f0:f0 + n, b * 256:(b + 1) * 256], n, 256, cW[f"d{l}"][g * NB + b, :, 0:n * 256])
                w3 = w_in["w_in"].rearrange("(k p) n -> p k n", p=128)
                for bi, (fam, off, j) in enumerate(s_blocks if part == 1 else []):
                    for k0 in range(0, DC, KCAST):
                        nk = min(KCAST, DC - k0)
                        cast_block(w3[:, k0:k0 + nk, off:off + 128], nk, 128, cWin_s[bi, :, k0 * 128:(k0 + nk) * 128])
                for bi, (fam, off, ncols, k0, nk, j) in enumerate(m_blocks if part == 1 else []):
                    cast_block(w3[:, k0:k0 + nk, off:off + ncols], nk, ncols, cWin_m[bi, :, 0:nk * ncols])
                wa3 = w_in["w_branch_attn"].rearrange("(k p) n -> p k n", p=128)
                wm3 = w_in["w_branch_mlstm"].rearrange("(k p) n -> p k n", p=128)
                wo3 = w_in["w_out"].rearrange("(k p) n -> p k n", p=128)
                for dc in range(DC if part == 2 else 0):
                    cast_block(wa3[:, :, dc * 128:(dc + 1) * 128], KA, 128, cWb[dc, :, 0:KA * 128])
                    cast_block(wm3[:, :, dc * 128:(dc + 1) * 128], KY - KA, 128, cWb[dc, :, KA * 128:KY * 128])
                    for k0 in range(0, DC, KCAST):
                        nk = min(KCAST, DC - k0)
                        cast_block(wo3[:, k0:k0 + nk, dc * 128:(dc + 1) * 128], nk, 128, cWout[dc, :, k0 * 128:(k0 + nk) * 128])
                return jobs

        def prepass():
            with contextlib.ExitStack() as pst:
                cur[0] = pst
                cb = make_caster()
                for jb in cast_jobs(1):
                    jb(cb)
                P.barrier()
            cur[0] = st

        class Core:
            pass

        def make_core(NS):
            K = Core()
            wslots = [(sb(f"wslot{i}", [128, SLOT], BF16), Buf(f"wslot{i}")) for i in range(NS)]

            class WQ:
                def __init__(self):
                    self.plan = []
                    self.k = 0
                    self.loaded = 0

                def add(self, src_ap, nelem):
                    self.plan.append((src_ap, nelem))

                def get_group(self, n):
                    idx = self.k
                    self.k += n
                    assert n <= NS
                    lim = min(len(self.plan), idx + NS)
                    while self.loaded < lim:
                        i = self.loaded
                        src, ne = self.plan[i]
                        tl, bf = wslots[i % NS]
                        P.dma("sp", (lambda e, tl=tl, src=src, ne=ne: e.dma_start(out=tl[:, 0:ne], in_=src)),
                              bf, reads=[cWbuf], writes=[bf])
                        self.loaded += 1
                    return [wslots[(idx + q) % NS] for q in range(n)]

                def get(self):
                    return self.get_group(1)[0]
            wq = WQ()
            K.wq = wq
            xT = sb("xT", [128, DC, 512], F32)
            B_xT = [Buf(f"xT{i}") for i in range(DC)]
            hT = sb("hT", [128, DC, 512], BF16)
            B_hT = [Buf(f"hT{i}") for i in range(DC)]
            actT = sb("actT", [128, GMAX, 512], BF16)
            B_act = [Buf(f"act{i}") for i in range(GMAX)]
            fpool = Rot([(sb(f"fp{i}", [128, 512], F32), Buf(f"fp{i}")) for i in range(7)])
            rstd_bc = sb("rstd_bc", [128, 512], F32)
            B_rstd = Buf("rstd")
            K.xT, K.B_xT, K.hT, K.B_hT, K.fpool = xT, B_xT, hT, B_hT, fpool
            K.rstd_bc, K.B_rstd = rstd_bc, B_rstd

            def rms_stats():
                pb, pbb = bankrot.next()
                for dc in range(DC):
                    t_, tb = fpool.next()
                    P.op("act", (lambda e, dc=dc, t_=t_: e.activation(out=t_[:], in_=xT[:, dc, :], func=AF.Square)),
                         reads=[B_xT[dc]], writes=[tb])
                    P.op("pe", (lambda e, dc=dc, t_=t_: e.matmul(pb[:], lhsT=ones_f[:], rhs=t_[:], start=(dc == 0), stop=(dc == DC - 1))),
                         reads=[tb, B_ones], writes=[pbb])
                t1, t1b = fpool.next()
                P.op("dve", (lambda e: e.tensor_scalar(out=t1[:], in0=pb[:], scalar1=1.0 / D, scalar2=NORM_EPS, op0=ALU.mult, op1=ALU.add)),
                     reads=[pbb], writes=[t1b])
                t2, t2b = fpool.next()
                P.op("act", (lambda e: e.activation(out=t2[:], in_=t1[:], func=AF.Sqrt)), reads=[t1b], writes=[t2b])
                P.op("dve", (lambda e: e.reciprocal(out=rstd_bc[:], in_=t2[:])), reads=[t2b], writes=[B_rstd])

            def rmsnorm_T(gi):
                rms_stats()
                for dc in range(DC):
                    P.op("dve", (lambda e, dc=dc: e.scalar_tensor_tensor(out=hT[:, dc, :], in0=xT[:, dc, :], scalar=gains[:, gi, dc:dc + 1],
                                                                          in1=rstd_bc[:], op0=ALU.mult, op1=ALU.mult)),
                         reads=[B_xT[dc], B_gains, B_rstd], writes=[B_hT[dc]])
            K.rms_stats, K.rmsnorm_T = rms_stats, rmsnorm_T

            def plan_ffn(l):
                for g, (f0, n) in enumerate(c.fgroups):
                    for j in range(n):
                        wq.add(cW[f"g{l}"][f0 + j, :, :], DC * 128)
                        wq.add(cW[f"u{l}"][f0 + j, :, :], DC * 128)
                    for b in range(NB):
                        wq.add(cW[f"d{l}"][g * NB + b, :, 0:n * 256], n * 256)

            def ffn(l):
                for g, (f0, n) in enumerate(c.fgroups):
                    for j in range(n):
                        pg, pgb = bankrot.next()
                        pu, pub = bankrot.next()
                        for (pt, ptb) in ((pg, pgb), (pu, pub)):
                            wt, wb = wq.get()

                            def mm(e, wt=wt, pt=pt):
                                ins = None
                                for dc in range(DC):
                                    ins = e.matmul(pt[:], lhsT=wt[:, dc * 128:(dc + 1) * 128], rhs=hT[:, dc, :],
                                                   start=(dc == 0), stop=(dc == DC - 1))
                                return ins
                            P.op("pe", mm, reads=[wb] + B_hT, writes=[ptb])
                        t_, tb = fpool.next()
                        P.op("act", (lambda e, t_=t_, pg=pg: e.activation(out=t_[:], in_=pg[:], func=AF.Silu)), reads=[pgb], writes=[tb])
                        P.op("dve", (lambda e, t_=t_, pu=pu, j=j: e.tensor_tensor(out=actT[:, j, :], in0=t_[:], in1=pu[:], op=ALU.mult)),
                             reads=[tb, pub], writes=[B_act[j]])
                    for b in range(NB):
                        wt, wb = wq.get()
                        pp = [bankrot.next(), bankrot.next()]

                        def mm(e, wt=wt, pp=pp, n=n):
                            ins = None
                            for dl in range(2):
                                for jj in range(n):
                                    ins = e.matmul(pp[dl][0][:], lhsT=wt[:, jj * 256 + dl * 128: jj * 256 + (dl + 1) * 128],
                                                   rhs=actT[:, jj, :], start=(jj == 0), stop=(jj == n - 1))
                            return ins
                        P.op("pe", mm, reads=[wb] + B_act[:n], writes=[pp[0][1], pp[1][1]])
                        for dl in range(2):
                            dc = 2 * b + dl
                            P.op("dve", (lambda e, dc=dc, pt=pp[dl][0]: e.scalar_tensor_tensor(out=xT[:, dc, :], in0=pt[:], scalar=0.5,
                                                                                               in1=xT[:, dc, :], op0=ALU.mult, op1=ALU.add)),
                                 reads=[pp[dl][1], B_xT[dc]], writes=[B_xT[dc]])
            K.plan_ffn, K.ffn = plan_ffn, ffn
            return K

        def phaseA():
            with contextlib.ExitStack() as pst:
                cur[0] = pst
                K = make_core(6)
                xT, B_xT, hT, B_hT, fpool, wq = K.xT, K.B_xT, K.hT, K.B_hT, K.fpool, K.wq
                XH = min(D, 2048)
                xtok = Rot([(sb(f"xtok{i}", [128, XH], F32), Buf(f"xtok{i}")) for i in range(2)])

                def load_xT(i):
                    for tcn in range(4):
                        r0 = i * 512 + tcn * 128
                        for h0 in range(0, D, XH):
                            xt, xb = xtok.next()
                            P.dma("sp", (lambda e, xt=xt, r0=r0, h0=h0: e.dma_start(out=xt[:], in_=x_in[r0:r0 + 128, h0:h0 + XH])), xb, writes=[xb])
                            for q0 in range(0, XH // 128, 4):
                                pb, pbb = bankrot.next()
                                nd = min(4, XH // 128 - q0)
                                dc0 = h0 // 128 + q0

                                def tr(e, xt=xt, pb=pb, q0=q0, nd=nd):
                                    ins = None
                                    for q in range(nd):
                                        ins = e.transpose(pb[:, q * 128:(q + 1) * 128], xt[:, (q0 + q) * 128:(q0 + q + 1) * 128], ident_f[:])
                                    return ins
                                P.op("pe", tr, reads=[xb, B_ident], writes=[pbb])
                                dst = xT[:, dc0:dc0 + nd, tcn * 128:(tcn + 1) * 128]
                                src = pb[:, 0:nd * 128].rearrange("p (q t) -> p q t", q=nd)
                                copy_evac(dst, src, [pbb], B_xT[dc0:dc0 + nd])

                B_xTst = Buf("xTstore")

                def store_xT(dst, key, i):
                    for dc in range(DC):
                        P.dma("pool", (lambda e, dc=dc: e.dma_start(out=dst[dc * 128:(dc + 1) * 128, i * 512:(i + 1) * 512], in_=xT[:, dc, :])),
                              B_xTst, reads=[B_xT[dc]], writes=[])
                    P.seal(B_xTst, B_xT, "r")
                    B_scr[key].w[P.sem_idx[id(B_xTst.dsem)]] = (B_xTst.dsem, P.dcount[id(B_xTst.dsem)])

                rc = sb("rope_c", [32, 4], F32)
                B_rc = Buf("rope_c")
                perm = sb("perm", [32, 32], F32)
                B_perm = Buf("perm")
                permb = sb("permb", [32, 32], F32)
                B_permb = Buf("permb")
                P.op("pool", lambda e: e.iota(out=rc[:, 2:3], pattern=[[0, 1]], base=0, channel_multiplier=1,
                                              allow_small_or_imprecise_dtypes=True), writes=[B_rc])
                P.op("dve", lambda e: e.tensor_scalar(out=rc[:, 3:4], in0=rc[:, 2:3], scalar1=15.5, scalar2=None, op0=ALU.is_gt),
                     reads=[B_rc], writes=[B_rc])
                P.op("dve", lambda e: e.tensor_scalar(out=rc[:, 1:2], in0=rc[:, 3:4], scalar1=2.0, scalar2=-1.0, op0=ALU.mult, op1=ALU.add),
                     reads=[B_rc], writes=[B_rc])
                P.op("dve", lambda e: e.scalar_tensor_tensor(out=rc[:, 2:3], in0=rc[:, 3:4], scalar=-16.0, in1=rc[:, 2:3], op0=ALU.mult, op1=ALU.add),
                     reads=[B_rc], writes=[B_rc])
                P.op("act", lambda e: e.activation(out=rc[:, 0:1], in_=rc[:, 2:3], func=AF.Exp, scale=-float(np.log(ROPE_THETA)) / 16.0),
                     reads=[B_rc], writes=[B_rc])
                P.op("pool", lambda e: e.memset(perm[:], 1.0), writes=[B_perm])
                P.op("pool", lambda e: e.memset(permb[:], 1.0), writes=[B_permb])
                P.op("pool", lambda e: e.affine_select(out=perm[:], in_=perm[:], pattern=[[-1, 32]], compare_op=ALU.is_equal,
                                                       fill=0.0, base=-16, channel_multiplier=1), reads=[B_perm], writes=[B_perm])
                P.op("pool", lambda e: e.affine_select(out=permb[:], in_=permb[:], pattern=[[-1, 32]], compare_op=ALU.is_equal,
                                                       fill=0.0, base=16, channel_multiplier=1), reads=[B_permb], writes=[B_permb])
                P.op("pool", lambda e: e.tensor_tensor(out=perm[:], in0=perm[:], in1=permb[:], op=ALU.add), reads=[B_perm, B_permb], writes=[B_perm])

                cossin = sb("cossin", [32, 2, 512], F32)
                cos32 = cossin[:, 0, :]
                sin32 = cossin[:, 1, :]
                B_cs = Buf("cossin")
                rti = sb("ropeti", [32, 512], I32)
                B_rt = Buf("ropetmp")
                TWO_PI = float(2 * np.pi)

                def rope_tables(i):
                    (u_, ub_), (a_, ab_), (b__, bb_) = fpool.next(), fpool.next(), fpool.next()
                    u, a, b_ = u_[0:32, :], a_[0:32, :], b__[0:32, :]
                    R, W_ = [B_rt, B_rc, ub_, ab_, bb_], [B_rt, ub_, ab_, bb_]
                    P.op("pool", lambda e: e.iota(out=u, pattern=[[1, 512]], base=i * 512, channel_multiplier=0,
                                                  allow_small_or_imprecise_dtypes=True), reads=R, writes=W_)
                    P.op("dve", lambda e: e.tensor_scalar(out=u, in0=u, scalar1=rc[:, 0:1], scalar2=1.0 / TWO_PI, op0=ALU.mult, op1=ALU.mult), reads=R, writes=W_)
                    P.op("dve", lambda e: e.tensor_copy(out=rti[:], in_=u), reads=R, writes=W_)
                    P.op("dve", lambda e: e.tensor_copy(out=a, in_=rti[:]), reads=R, writes=W_)
                    P.op("dve", lambda e: e.tensor_tensor(out=u, in0=u, in1=a, op=ALU.subtract), reads=R, writes=W_)

                    def wrap(t):
                        P.op("dve", lambda e: e.tensor_scalar(out=a, in0=t, scalar1=0.5, scalar2=None, op0=ALU.is_gt), reads=R, writes=W_)
                        P.op("dve", lambda e: e.tensor_tensor(out=t, in0=t, in1=a, op=ALU.subtract), reads=R, writes=W_)
                        P.op("dve", lambda e: e.tensor_scalar(out=a, in0=t, scalar1=-0.5, scalar2=None, op0=ALU.is_lt), reads=R, writes=W_)
                        P.op("dve", lambda e: e.tensor_tensor(out=t, in0=t, in1=a, op=ALU.add), reads=R, writes=W_)
                    wrap(u)
                    P.op("act", lambda e: e.activation(out=sin32, in_=u, func=AF.Sin, scale=TWO_PI), reads=R + [B_cs], writes=[B_cs])
                    P.op("dve", lambda e: e.tensor_scalar(out=sin32, in0=sin32, scalar1=rc[:, 1:2], scalar2=None, op0=ALU.mult), reads=[B_cs, B_rc], writes=[B_cs])
                    P.op("dve", lambda e: e.tensor_scalar(out=b_, in0=u, scalar1=0.25, scalar2=None, op0=ALU.add), reads=R, writes=W_)
                    wrap(b_)
                    P.op("act", lambda e: e.activation(out=cos32, in_=b_, func=AF.Sin, scale=TWO_PI), reads=R + [B_cs], writes=[B_cs])

                stb = Rot([(sb(f"stb{i}", [128, 512], BF16), Buf(f"stb{i}")) for i in range(4)])
                q32r = Rot([(sb(f"q32_{i}", [32, 512], F32), Buf(f"q32_{i}")) for i in range(2)])

                def plan_win():
                    for bi in range(len(s_blocks)):
                        wq.add(cWin_s[bi, :, :], DC * 128)
                    for bi, (fam, off, ncols, k0, nk, j) in enumerate(m_blocks):
                        wq.add(cWin_m[bi, :, 0:nk * ncols], nk * ncols)

                def inproj(i):
                    cols = slice(i * 512, (i + 1) * 512)
                    for bi, (fam, off, j) in enumerate(s_blocks):
                        wt, wb = wq.get()
                        pb, pbb = bankrot.next()

                        def mm(e, wt=wt, pb=pb):
                            ins = None
                            for dc in range(DC):
                                ins = e.matmul(pb[:], lhsT=wt[:, dc * 128:(dc + 1) * 128], rhs=hT[:, dc, :], start=(dc == 0), stop=(dc == DC - 1))
                            return ins
                        if "nostat" in dbg_names or (("norot" in dbg_names) and fam in ("qa", "ka")):
                            continue
                        P.op("pe", mm, reads=[wb] + B_hT, writes=[pbb])
                        if fam in ("qa", "ka"):
                            o_t, o_b = stb.next()
                            q_t, q_b = q32r.next()
                            P.op("act", (lambda e, o_t=o_t, pb=pb: e.copy(out=o_t[:], in_=pb[:])), reads=[pbb], writes=[o_b])
                            P.op("dve", (lambda e, q_t=q_t, pb=pb: e.tensor_copy(out=q_t[:], in_=pb[0:32, :])), reads=[pbb], writes=[q_b])
                            t_, tb = fpool.next()
                            P.dma("pool", (lambda e, t_=t_, q_t=q_t: e.dma_start(out=t_[0:16, :], in_=q_t[16:32, :])), tb, reads=[q_b], writes=[tb])
                            P.dma("pool", (lambda e, t_=t_, q_t=q_t: e.dma_start(out=t_[16:32, :], in_=q_t[0:16, :])), tb, reads=[q_b], writes=[tb])
                            P.op("dve", (lambda e, t_=t_: e.tensor_tensor(out=t_[0:32, :], in0=t_[0:32, :], in1=sin32, op=ALU.mult)),
                                 reads=[tb, B_cs], writes=[tb])
                            P.op("dve", (lambda e, q_t=q_t: e.tensor_tensor(out=q_t[:], in0=q_t[:], in1=cos32, op=ALU.mult)),
                                 reads=[q_b, B_cs], writes=[q_b])
                            P.op("dve", (lambda e, o_t=o_t, q_t=q_t, t_=t_: e.tensor_tensor(out=o_t[0:32, :], in0=q_t[:], in1=t_[0:32, :], op=ALU.add)),
                                 reads=[q_b, tb], writes=[o_b])
                            dstT = QaT if fam == "qa" else KaT
                            scr_store("pool", (lambda e, o_t=o_t, dstT=dstT, j=j: e.dma_start(out=dstT[j * 128:(j + 1) * 128, cols], in_=o_t[:])),
                                      o_b, "QaT" if fam == "qa" else "KaT")
                        elif fam == "va":
                            o_t, o_b = stb.next()
                            copy_evac(o_t[:], pb[:], [pbb], [o_b])
                            scr_store("pool", (lambda e, o_t=o_t, j=j: e.dma_start(out=VaT[j * 128:(j + 1) * 128, cols], in_=o_t[:])), o_b, "VaT")
                        elif fam in ("qm", "km"):
                            o_t, o_b = fpool.next()
                            copy_evac(o_t[:], pb[:], [pbb], [o_b])
                            r0 = j * 128 + (MW if fam == "km" else 0)
                            scr_store("pool", (lambda e, o_t=o_t, r0=r0: e.dma_start(out=QKm[r0:r0 + 128, cols], in_=o_t[:])), o_b, "QKm")
                        else:
                            o_t, o_b = fpool.next()
                            P.op("act", (lambda e, o_t=o_t, pb=pb: e.activation(out=o_t[:], in_=pb[:], func=AF.Sigmoid)), reads=[pbb], writes=[o_b])
                            r0 = j * 128 + (D if fam == "gm" else 0)
                            scr_store("pool", (lambda e, o_t=o_t, r0=r0: e.dma_start(out=GT[r0:r0 + 128, cols], in_=o_t[:])), o_b, "GT")
                    mi = 0
                    while mi < len(m_blocks):
                        fam, off, ncols, k0, nk, j = m_blocks[mi]
                        grp = [m_blocks[mi]]
                        while mi + len(grp) < len(m_blocks) and m_blocks[mi + len(grp)][0] == fam and m_blocks[mi + len(grp)][5] == j:
                            grp.append(m_blocks[mi + len(grp)])
                        assert len(grp) <= 5
                        slots = wq.get_group(len(grp))
                        for tcn in range(4 if "nomov" not in dbg_names else 0):
                            pb, pbb = bankrot.next()

                            def mm(e, pb=pb, tcn=tcn, grp=grp, slots=slots, ncols=ncols):
                                ins = None
                                for blk, (wt, wb) in zip(grp, slots):
                                    _, _, _, k0_, nk_, _ = blk
                                    for kk in range(nk_):
                                        dc = k0_ + kk
                                        ins = e.matmul(pb[:, 0:ncols], lhsT=hT[:, dc, tcn * 128:(tcn + 1) * 128],
                                                       rhs=wt[:, kk * ncols:(kk + 1) * ncols], start=(dc == 0), stop=(dc == DC - 1))
                                return ins
                            P.op("pe", mm, reads=[wb for (_, wb) in slots] + B_hT, writes=[pbb])
                            rows = slice(i * 512 + tcn * 128, i * 512 + (tcn + 1) * 128)
                            c0 = j * 512
                            if fam == "vm":
                                o_t, o_b = stb.next()
                                copy_evac(o_t[:, 0:ncols], pb[:, 0:ncols], [pbb], [o_b])
                                scr_store("pool", (lambda e, o_t=o_t, rows=rows, c0=c0, ncols=ncols: e.dma_start(out=Vm[rows, c0:c0 + ncols], in_=o_t[:, 0:ncols])), o_b, "Vm")
                            elif fam == "om":
                                o_t, o_b = fpool.next()
                                P.op("act", (lambda e, o_t=o_t, pb=pb, ncols=ncols: e.activation(out=o_t[:, 0:ncols], in_=pb[:, 0:ncols], func=AF.Sigmoid)),
                                     reads=[pbb], writes=[o_b])
                                scr_store("pool", (lambda e, o_t=o_t, rows=rows, c0=c0, ncols=ncols: e.dma_start(out=Om[rows, c0:c0 + ncols], in_=o_t[:, 0:ncols])), o_b, "Om")
                            else:
                                o_t, o_b = fpool.next()
                                copy_evac(o_t[:, 0:ncols], pb[:, 0:ncols], [pbb], [o_b])
                                scr_store("pool", (lambda e, o_t=o_t, rows=rows, ncols=ncols: e.dma_start(out=MG[rows, 0:ncols], in_=o_t[:, 0:ncols])), o_b, "MG")
                        mi += len(grp)

                for i in range(NT):
                    K.plan_ffn(1)
                    plan_win()
                lvl = 9
                for q_ in range(9):
                    if f"pa{q_}" in dbg_names:
                        lvl = q_
                for i in range(NT):
                    load_xT(i)
                    if lvl >= 2:
                        K.rmsnorm_T(0)
                    if lvl >= 3:
                        K.ffn(1)
                    store_xT(X1T, "X1T", i)
                    if lvl >= 4:
                        K.rmsnorm_T(1)
                    if lvl >= 5:
                        rope_tables(i)
                    if lvl >= 6:
                        inproj(i)
                P.barrier()
            cur[0] = st

        def conv_pass():
            with contextlib.ExitStack() as pst:
                cur[0] = pst
                CT = 1024
                NCC = 2 * MW // 128
                cwt = sb("cwt", [128, NCC, 5], F32)
                cbt = sb("cbt", [128, NCC], F32)
                B_cw = Buf("cw")
                for j in range(5):
                    load_colvec(cwt[:, :, j], B_cw, w_in["mlstm_conv_w"][j:j + 1, :], NCC)
                load_colvec(cbt[:], B_cw, w_in["mlstm_conv_b"], NCC)
                U = Rot([(sb(f"cu{i}", [128, CT + 4], F32), Buf(f"cu{i}")) for i in range(2)])
                ACC = Rot([(sb(f"ca{i}", [128, CT], F32), Buf(f"ca{i}")) for i in range(2)])
                OUT = Rot([(sb(f"co{i}", [128, CT], BF16), Buf(f"co{i}")) for i in range(2)])
                QmCv = QmC.rearrange("c p k t -> p c k t")
                KmCv = KmC.rearrange("c p k t -> p c k t")
                for cc in range(NCC):
                    is_k = cc >= MW // 128
                    kidx = cc - (MW // 128 if is_k else 0)
                    for t0 in range(0, T, CT):
                        ut, ub = U.next()
                        lo, hi = max(t0 - 2, 0), min(t0 + CT + 2, T)
                        if t0 == 0:
                            P.op("pool", (lambda e, ut=ut: e.memset(ut[:, 0:2], 0.0)), writes=[ub])
                        if t0 + CT == T:
                            P.op("pool", (lambda e, ut=ut: e.memset(ut[:, CT + 2:CT + 4], 0.0)), writes=[ub])
                        P.dma("sp", (lambda e, ut=ut, lo=lo, hi=hi, t0=t0, cc=cc: e.dma_start(out=ut[:, lo - (t0 - 2):hi - (t0 - 2)],
                                                                                             in_=QKm[cc * 128:(cc + 1) * 128, lo:hi])),
                              ub, reads=[B_scr["QKm"]], writes=[ub])
                        at, ab = ACC.next()
                        P.op("dve", (lambda e, at=at, ut=ut, cc=cc: e.tensor_scalar(out=at[:], in0=ut[:, 0:CT], scalar1=cwt[:, cc, 0:1],
                                                                                    scalar2=cbt[:, cc:cc + 1], op0=ALU.mult, op1=ALU.add)),
                             reads=[ub, B_cw], writes=[ab])
                        for j in range(1, 5):
                            P.op("dve", (lambda e, at=at, ut=ut, cc=cc, j=j: e.scalar_tensor_tensor(out=at[:], in0=ut[:, j:j + CT], scalar=cwt[:, cc, j:j + 1],
                                                                                                      in1=at[:], op0=ALU.mult, op1=ALU.add)),
                                 reads=[ub, B_cw, ab], writes=[ab])
                        ot, ob = OUT.next()
                        if not is_k:
                            P.op("act", (lambda e, at=at, ot=ot: e.activation(out=ot[:], in_=at[:], func=AF.Silu)), reads=[ab], writes=[ob])
                        else:
                            P.op("act", (lambda e, at=at: e.activation(out=at[:], in_=at[:], func=AF.Silu)), reads=[ab], writes=[ab])
                            P.op("pool", (lambda e, at=at, ot=ot: e.tensor_scalar(out=ot[:], in0=at[:], scalar1=1.0 / 16.0, scalar2=None, op0=ALU.mult)),
                                 reads=[ab], writes=[ob])
                        dstv = (KmCv if is_k else QmCv)[:, t0 // 128:(t0 + CT) // 128, kidx, :]
                        scr_store("pool", (lambda e, ot=ot, dstv=dstv: e.dma_start(out=dstv, in_=ot[:].rearrange("p (c t) -> p c t", t=128))),
                                  ob, "KmC" if is_k else "QmC")
                P.barrier()
            cur[0] = st

        def attention():
            with contextlib.ExitStack() as pst:
                cur[0] = pst
                mask3 = sb("mask3", [128, 3, 128], BF16)
                mtmp = sb("mtmp", [128, 3, 128], F32)
                B_mask = Buf("mask3")
                P.op("pool", lambda e: e.memset(mtmp[:], 1.0), writes=[B_mask])
                P.op("pool", lambda e: e.affine_select(out=mtmp[:, 0, :], in_=mtmp[:, 0, :], pattern=[[-1, 128]], compare_op=ALU.is_ge,
                                                       fill=0.0, base=-64, channel_multiplier=1), reads=[B_mask], writes=[B_mask])
                P.op("pool", lambda e: e.affine_select(out=mtmp[:, 1, :], in_=mtmp[:, 1, :], pattern=[[-1, 128]], compare_op=ALU.is_ge,
                                                       fill=0.0, base=64, channel_multiplier=1), reads=[B_mask], writes=[B_mask])
                P.op("pool", lambda e: e.affine_select(out=mtmp[:, 1, :], in_=mtmp[:, 1, :], pattern=[[1, 128]], compare_op=ALU.is_ge,
                                                       fill=0.0, base=64, channel_multiplier=-1), reads=[B_mask], writes=[B_mask])
                P.op("pool", lambda e: e.affine_select(out=mtmp[:, 2, :], in_=mtmp[:, 2, :], pattern=[[1, 128]], compare_op=ALU.is_ge,
                                                       fill=0.0, base=-64, channel_multiplier=-1), reads=[B_mask], writes=[B_mask])
                P.op("pool", lambda e: e.tensor_copy(out=mask3[:], in_=mtmp[:]), reads=[B_mask], writes=[B_mask])
                QKV = Rot([((sb(f"aq{i}", [128, T], BF16), sb(f"ak{i}", [128, T], BF16), sb(f"av{i}", [128, T], BF16)), Buf(f"aqkv{i}")) for i in range(2)])
                ACC = sb("aacc", [128, 2, T], F32)
                B_acc = Buf("aacc")
                Vrm = sb("vrm", [128, T // 128, 128], BF16)
                B_vrm = Buf("vrm")
                Pt = Rot([(sb(f"apt{i}", [128, 3, 128], BF16), Buf(f"apt{i}")) for i in range(5)])
                yout = sb("ayout", [128, T], BF16)
                B_yout = Buf("ayout")
                scale = float(128 ** -0.5)
                for h in range(HA):
                    for g, d in enumerate((1, 4, 16)):
                        hh = g * HA + h
                        (qt, kt, vt), qb = QKV.next()
                        for tl, src in ((qt, QaT), (kt, KaT), (vt, VaT)):
                            P.dma("sp", (lambda e, tl=tl, src=src, hh=hh: e.dma_start(out=tl[:], in_=src[hh * 128:(hh + 1) * 128, :])),
                                  qb, reads=[B_scr["QaT"], B_scr["KaT"], B_scr["VaT"]], writes=[qb])
                        S = T // d
                        nb = S // 128

                        def cols(r, b, d=d):
                            s0 = r + d * 128 * b
                            return slice(s0, s0 + 127 * d + 1, d)
                        allc = [(r, b) for r in range(d) for b in range(nb)]
                        for c4 in range(0, len(allc), 4):
                            pb, pbb = bankrot.next()
                            pbv = pb[:].bitcast(BF16)

                            def tr(e, c4=c4, pbv=pbv, vt=vt, allc=allc, cols=cols):
                                ins = None
                                for q in range(4):
                                    r, b = allc[c4 + q]
                                    ins = e.transpose(pbv[:, q * 128:(q + 1) * 128], vt[:, cols(r, b)], ident_b[:])
                                return ins
                            P.op("pe", tr, reads=[qb, B_identb], writes=[pbb])
                            copy_evac(Vrm[:, c4:c4 + 4, :], pbv[:, 0:512].rearrange("p (q t) -> p q t", q=4), [pbb], [B_vrm])

                        def s1(r, b, qt=qt, kt=kt, nb=nb, cols=cols):
                            cks = [cb for cb in (b - 1, b, b + 1) if 0 <= cb < nb]
                            j0, nj = cks[0] - b + 1, len(cks)
                            pS, pSb = bankrot.next()

                            def mm(e):
                                ins = None
                                for cb in cks:
                                    j = cb - b + 1
                                    ins = e.matmul(pS[:, j * 128:(j + 1) * 128], lhsT=kt[:, cols(r, cb)], rhs=qt[:, cols(r, b)], start=True, stop=True)
                                return ins
                            P.op("pe", mm, reads=[qb], writes=[pSb])
                            pt, ptb = Pt.next()
                            P.op("act", (lambda e: e.activation(out=pt[:, j0:j0 + nj, :], in_=pS[:, j0 * 128:(j0 + nj) * 128].rearrange("p (j t) -> p j t", t=128),
                                                                func=AF.Exp, scale=scale)), reads=[pSb], writes=[ptb])
                            P.op("pool", (lambda e: e.tensor_tensor(out=pt[:, j0:j0 + nj, :], in0=pt[:, j0:j0 + nj, :], in1=mask3[:, j0:j0 + nj, :], op=ALU.mult)),
                                 reads=[ptb, B_mask], writes=[ptb])
                            return (r, b, cks, pt, ptb)

                        def s2(stt, g=g, nb=nb, cols=cols):
                            r, b, cks, pt, ptb = stt
                            pO, pOb = bankrot.next()

                            def mm(e):
                                ins = None
                                for ii, cb in enumerate(cks):
                                    j = cb - b + 1
                                    ins = e.matmul(pO[:, 0:128], lhsT=Vrm[:, r * nb + cb, :], rhs=pt[:, j, :], start=(ii == 0), stop=(ii == len(cks) - 1))
                                for ii, cb in enumerate(cks):
                                    j = cb - b + 1
                                    ins = e.matmul(pO[:, 128:256], lhsT=ones_b[:], rhs=pt[:, j, :], start=(ii == 0), stop=(ii == len(cks) - 1))
                                return ins
                            P.op("pe", mm, reads=[ptb, B_vrm, B_onesb], writes=[pOb])
                            dst = ACC[:, :, cols(r, b)]
                            src = pO[:, 0:256].rearrange("p (a t) -> p a t", a=2)
                            if g == 0:
                                P.op("dve", (lambda e: e.tensor_copy(out=dst, in_=src)), reads=[pOb], writes=[B_acc])
                            else:
                                P.op("dve", (lambda e: e.tensor_tensor(out=dst, in0=dst, in1=src, op=ALU.add)), reads=[pOb, B_acc], writes=[B_acc])
                        pend = []
                        for (r, b) in allc:
                            pend.append(s1(r, b))
                            if len(pend) > 3:
                                s2(pend.pop(0))
                        while pend:
                            s2(pend.pop(0))
                    P.op("dve", (lambda e: e.reciprocal(out=ACC[:, 1, :], in_=ACC[:, 1, :])), reads=[B_acc], writes=[B_acc])
                    P.op("dve", (lambda e: e.tensor_tensor(out=yout[:], in0=ACC[:, 0, :], in1=ACC[:, 1, :], op=ALU.mult)), reads=[B_acc], writes=[B_yout])
                    scr_store("pool", (lambda e, h=h: e.dma_start(out=YT[h * 128:(h + 1) * 128, :], in_=yout[:])), B_yout, "YT")
                P.barrier()
            cur[0] = st

        def mlstm():
            with contextlib.ExitStack() as pst:
                cur[0] = pst
                NCOL = NCH * HM
                MGc = sb("MGc", [128, NCH, 4 * HM], F32)
                B_mg = Buf("MGc")
                MGv = MG.rearrange("(c p) g -> p c g", p=128)
                for c0 in range(0, NCH, 8):
                    P.dma("sp", (lambda e, c0=c0: e.dma_start(out=MGc[:, c0:c0 + 8, :], in_=MGv[:, c0:c0 + 8, :])), B_mg, reads=[B_scr["MG"]], writes=[B_mg])
                gb = sb("gbias", [128, 4 * HM], F32)
                B_gb = Buf("gbias")
                P.dma("sp", lambda e: e.dma_start(out=gb[:], in_=w_in["mlstm_gate_bias"][0:1, :].partition_broadcast(128)), B_gb, writes=[B_gb])
                P.op("dve", lambda e: e.tensor_tensor(out=MGc[:], in0=MGc[:], in1=gb[:].unsqueeze(1).to_broadcast([128, NCH, 4 * HM]), op=ALU.add),
                     reads=[B_mg, B_gb], writes=[B_mg])
                ntri = sb("ntri", [128, 2, 128], F32)
                cmask = sb("cmask", [128, 2, 128], F32)
                nones = sb("nones", [128, 128], F32)
                B_tri = Buf("tri")
                P.op("pool", lambda e: e.memset(ntri[:], -1.0), writes=[B_tri])
                P.op("pool", lambda e: e.memset(cmask[:], 1.0), writes=[B_tri])
                P.op("pool", lambda e: e.memset(nones[:], -1.0), writes=[B_tri])
                for (tl) in (ntri, cmask):
                    P.op("pool", (lambda e, tl=tl: e.affine_select(out=tl[:, 0, :], in_=tl[:, 0, :], pattern=[[1, 128]], compare_op=ALU.is_ge,
                                                                   fill=0.0, base=0, channel_multiplier=-1)), reads=[B_tri], writes=[B_tri])
                    P.op("pool", (lambda e, tl=tl: e.affine_select(out=tl[:, 1, :], in_=tl[:, 1, :], pattern=[[-1, 128]], compare_op=ALU.is_ge,
                                                                   fill=0.0, base=0, channel_multiplier=1)), reads=[B_tri], writes=[B_tri])
                G = {}
                for nm in ("EB", "ET", "W", "W2"):
                    for di in range(2):
                        G[(nm, di)] = sb(f"g{nm}{di}", [128, NCOL], F32)
                B_G = Buf("gatevecs")
                spt = sb("spt", [128, NCH, HM], F32)
                B_sp = Buf("spt")
                for di, (ioff, foff) in enumerate(((0, HM), (2 * HM, 3 * HM))):
                    P.op("act", (lambda e, foff=foff: e.activation(out=spt[:], in_=MGc[:, :, foff:foff + HM], func=AF.Exp, scale=-1.0)),
                         reads=[B_mg, B_sp], writes=[B_sp])
                    P.op("act", (lambda e: e.activation(out=spt[:], in_=spt[:], func=AF.Ln, bias=1.0)), reads=[B_sp], writes=[B_sp])
                    spf = spt[:].rearrange("p c h -> p (c h)")
                    for g0 in range(0, NCOL, 512):
                        n = min(512, NCOL - g0)
                        ncg = n // HM
                        cg0 = g0 // HM
                        pb, pbb = bankrot.next()
                        pt_, ptb = bankrot.next()
                        P.op("pe", (lambda e, pb=pb, di=di, g0=g0, n=n, spf=spf: e.matmul(pb[:, 0:n], lhsT=ntri[:, di, :], rhs=spf[:, g0:g0 + n], start=True, stop=True)),
                             reads=[B_sp, B_tri], writes=[pbb])
                        P.op("pe", (lambda e, pt_=pt_, g0=g0, n=n, spf=spf: e.matmul(pt_[:, 0:n], lhsT=nones[:], rhs=spf[:, g0:g0 + n], start=True, stop=True)),
                             reads=[B_sp, B_tri], writes=[ptb])
                        EB, ET, W, W2 = G[("EB", di)], G[("ET", di)], G[("W", di)], G[("W2", di)]
                        P.op("act", (lambda e, EB=EB, pb=pb, g0=g0, n=n: e.activation(out=EB[:, g0:g0 + n], in_=pb[:, 0:n], func=AF.Exp)), reads=[pbb], writes=[B_G])
                        P.op("act", (lambda e, ET=ET, pt_=pt_, g0=g0, n=n: e.activation(out=ET[:, g0:g0 + n], in_=pt_[:, 0:n], func=AF.Exp)), reads=[ptb], writes=[B_G])
                        P.op("dve", (lambda e, W=W, pb=pb, g0=g0, n=n, cg0=cg0, ncg=ncg, ioff=ioff: e.tensor_tensor(
                            out=W[:, g0:g0 + n].rearrange("p (c h) -> p c h", h=HM), in0=MGc[:, cg0:cg0 + ncg, ioff:ioff + HM],
                            in1=pb[:, 0:n].rearrange("p (c h) -> p c h", h=HM), op=ALU.subtract)), reads=[pbb, B_mg, B_G], writes=[B_G])
                        P.op("act", (lambda e, W=W, g0=g0, n=n: e.activation(out=W[:, g0:g0 + n], in_=W[:, g0:g0 + n], func=AF.Exp)), reads=[B_G], writes=[B_G])
                        P.op("dve", (lambda e, W=W, W2=W2, ET=ET, g0=g0, n=n: e.tensor_tensor(out=W2[:, g0:g0 + n], in0=W[:, g0:g0 + n], in1=ET[:, g0:g0 + n], op=ALU.mult)),
                             reads=[B_G], writes=[B_G])

                if upto >= 4:
                  with contextlib.ExitStack() as p1st:
                    cur[0] = p1st
                    QT = Rot([(sb(f"mq{i}", [128, 2 * HM, 128], BF16), Buf(f"mq{i}")) for i in range(4)])
                    KT = Rot([(sb(f"mk{i}", [128, 2 * HM, 128], BF16), Buf(f"mk{i}")) for i in range(4)])
                    VA = []
                    for i in range(4):
                        t_ = sb(f"mv{i}", [128, HM, 257], BF16)
                        b_ = Buf(f"mv{i}")
                        P.op("pool", (lambda e, t_=t_: e.memset(t_[:], 1.0)), writes=[b_])
                        VA.append((t_, b_))
                    VA = Rot(VA)
                    Cst = sb("Cst", [128, 2 * HM, 2, 257], F32)
                    Cbf = sb("Cbf", [128, 2 * HM, 2, 257], BF16)
                    B_C = [Buf(f"C{i}") for i in range(2 * HM)]
                    B_Cb = [Buf(f"Cb{i}") for i in range(2 * HM)]
                    P.op("pool", lambda e: e.memset(Cst[:], 0.0), writes=B_C)
                    P.op("pool", lambda e: e.memset(Cbf[:], 0.0), writes=B_Cb)
                    PW = Rot([(sb(f"mpw{i}", [128, 128], BF16), Buf(f"mpw{i}")) for i in range(3)])
                    KW = Rot([(sb(f"mkw{i}", [128, 256], BF16), Buf(f"mkw{i}")) for i in range(3)])
                    HS = Rot([(sb(f"mhs{i}", [128, 257], F32), Buf(f"mhs{i}")) for i in range(4)])
                    cb2 = make_caster()
                    jobs2 = cast_jobs(2)
                    per_step = -(-len(jobs2) // NCH)
                    for stp in range(NCH):
                        for jb in jobs2[stp * per_step:(stp + 1) * per_step]:
                            jb(cb2)
                        for di, ch in enumerate((stp, NCH - 1 - stp)):
                            qt, qb = QT.next()
                            kt, kb = KT.next()
                            va, vb = VA.next()
                            P.dma("sp", (lambda e, qt=qt, ch=ch: e.dma_start(out=qt[:], in_=QmC[ch])), qb, reads=[B_scr["QmC"]], writes=[qb])
                            P.dma("sp", (lambda e, kt=kt, ch=ch: e.dma_start(out=kt[:], in_=KmC[ch])), kb, reads=[B_scr["KmC"]], writes=[kb])
                            P.dma("sp", (lambda e, va=va, ch=ch: e.dma_start(out=va[:, :, 0:256], in_=Vm[ch * 128:(ch + 1) * 128, :].rearrange("p (h e) -> p h e", e=256))),
                                  vb, reads=[B_scr["Vm"]], writes=[vb])
                            W, W2, ET = G[("W", di)], G[("W2", di)], G[("ET", di)]
                            for hd in range(HM):
                                col = ch * HM + hd
                                sidx = di * HM + hd
                                pS, pSb = bankrot.next()

                                def mmS(e, pS=pS, kt=kt, qt=qt, hd=hd):
                                    ins = None
                                    for dcn in range(2):
                                        ins = e.matmul(pS[:, 0:128], lhsT=kt[:, 2 * hd + dcn, :], rhs=qt[:, 2 * hd + dcn, :], start=(dcn == 0), stop=(dcn == 1))
                                    return ins
                                P.op("pe", mmS, reads=[kb, qb], writes=[pSb])
                                pw, pwb = PW.next()
                                P.op("dve", (lambda e, pw=pw, pS=pS, W=W, col=col, di=di: e.scalar_tensor_tensor(out=pw[:], in0=pS[:, 0:128], scalar=W[:, col:col + 1],
                                                                                                                  in1=cmask[:, di, :], op0=ALU.mult, op1=ALU.mult)),
                                     reads=[pSb, B_G, B_tri], writes=[pwb])
                                pK, pKb = bankrot.next()
                                pKv = pK[:].bitcast(BF16)

                                def trK(e, pKv=pKv, kt=kt, hd=hd):
                                    ins = None
                                    for dcn in range(2):
                                        ins = e.transpose(pKv[:, dcn * 128:(dcn + 1) * 128], kt[:, 2 * hd + dcn, :], ident_b[:])
                                    return ins
                                P.op("pe", trK, reads=[kb, B_identb], writes=[pKb])
                                kw, kwb = KW.next()
                                P.op("act", (lambda e, kw=kw, pKv=pKv, W2=W2, col=col: e.activation(out=kw[:], in_=pKv[:, 0:256], func=AF.Copy, scale=W2[:, col:col + 1])),
                                     reads=[pKb, B_G], writes=[kwb])
                                pA, pAb = bankrot.next()

                                def mmA(e, pA=pA, pw=pw, va=va, qt=qt, hd=hd, sidx=sidx):
                                    e.matmul(pA[:, 0:257], lhsT=pw[:], rhs=va[:, hd, :], start=True, stop=False)
                                    ins = None
                                    for dcn in range(2):
                                        ins = e.matmul(pA[:, 0:257], lhsT=qt[:, 2 * hd + dcn, :], rhs=Cbf[:, sidx, dcn, :], start=False, stop=(dcn == 1))
                                    return ins
                                P.op("pe", mmA, reads=[pwb, vb, qb, B_Cb[sidx]], writes=[pAb])
                                hs, hsb = HS.next()
                                copy_evac(hs[:], pA[:, 0:257], [pAb], [hsb])
                                scr_store("pool", (lambda e, hs=hs, di=di, ch=ch, hd=hd: e.dma_start(out=HRAW[di, ch * 128:(ch + 1) * 128, hd, :], in_=hs[:])), hsb, "HRAW")
                                pU = [bankrot.next(), bankrot.next()]

                                def mmU(e, pU=pU, kw=kw, va=va, hd=hd):
                                    ins = None
                                    for dcn in range(2):
                                        ins = e.matmul(pU[dcn][0][:, 0:257], lhsT=kw[:, dcn * 128:(dcn + 1) * 128], rhs=va[:, hd, :], start=True, stop=True)
                                    return ins
                                P.op("pe", mmU, reads=[kwb, vb], writes=[pU[0][1], pU[1][1]])
                                for dcn in range(2):
                                    P.op("dve", (lambda e, dcn=dcn, pu=pU[dcn][0], sidx=sidx, ET=ET, col=col: e.scalar_tensor_tensor(
                                        out=Cst[:, sidx, dcn, :], in0=Cst[:, sidx, dcn, :], scalar=ET[:, col:col + 1], in1=pu[:, 0:257], op0=ALU.mult, op1=ALU.add)),
                                        reads=[pU[dcn][1], B_G, B_C[sidx]], writes=[B_C[sidx]])
                                P.op("act", (lambda e, sidx=sidx: e.copy(out=Cbf[:, sidx, :, :], in_=Cst[:, sidx, :, :])), reads=[B_C[sidx]], writes=[B_Cb[sidx]])
                    P.barrier()

                cur[0] = pst
                if upto >= 5:
                  with contextlib.ExitStack() as p2st:
                    cur[0] = p2st
                    hnw = sb("hnw", [128, MW], F32)
                    B_hnw = Buf("hnw")
                    P.dma("sp", lambda e: e.dma_start(out=hnw[:], in_=w_in["mlstm_head_norm"][0:1, :].partition_broadcast(128)), B_hnw, writes=[B_hnw])
                    RF = Rot([(sb(f"rf{i}", [128, HM, 257], F32), Buf(f"rf{i}")) for i in range(2)])
                    RB = Rot([(sb(f"rb{i}", [128, HM, 257], F32), Buf(f"rb{i}")) for i in range(2)])
                    OM = Rot([(sb(f"om{i}", [128, MW], F32), Buf(f"om{i}")) for i in range(2)])
                    HH = Rot([(sb(f"hh{i}", [128, HM, 256], F32), Buf(f"hh{i}")) for i in range(2)])
                    TT_ = Rot([(sb(f"ht{i}", [128, HM, 256], F32), Buf(f"ht{i}")) for i in range(2)])
                    YB = Rot([(sb(f"yb{i}", [128, MW], BF16), Buf(f"yb{i}")) for i in range(2)])
                    YS = Rot([(sb(f"ys{i}", [128, MW // 128, 128], BF16), Buf(f"ys{i}")) for i in range(2)])
                    SM = Rot([(sb(f"sm{i}", [128, 8, HM], F32), Buf(f"sm{i}")) for i in range(2)])
                    def do_chunk(ch):
                        rows = slice(ch * 128, (ch + 1) * 128)
                        rf, rfb = RF.next()
                        rb, rbb = RB.next()
                        om, omb = OM.next()
                        P.dma("sp", (lambda e, rf=rf, rows=rows: e.dma_start(out=rf[:], in_=HRAW[0, rows, :, :])), rfb, reads=[B_scr["HRAW"]], writes=[rfb])
                        P.dma("sp", (lambda e, rb=rb, rows=rows: e.dma_start(out=rb[:], in_=HRAW[1, rows, :, :])), rbb, reads=[B_scr["HRAW"]], writes=[rbb])
                        P.dma("sp", (lambda e, om=om, rows=rows: e.dma_start(out=om[:], in_=Om[rows, :])), omb, reads=[B_scr["Om"]], writes=[omb])
                        sm, smb = SM.next()
                        cs = slice(ch * HM, (ch + 1) * HM)
                        for di, (rr_, rrb) in enumerate(((rf, rfb), (rb, rbb))):
                            EB = G[("EB", di)]
                            den, cf = sm[:, 2 * di, :], sm[:, 2 * di + 1, :]
                            P.op("dve", (lambda e, den=den, rr_=rr_, EB=EB: e.tensor_tensor(out=den, in0=rr_[:, :, 256], in1=EB[:, cs], op=ALU.mult)), reads=[rrb, B_G, smb], writes=[smb])
                            neg = sm[:, 7, :]
                            P.op("dve", (lambda e, den=den, neg=neg: e.tensor_scalar(out=neg, in0=den, scalar1=-1.0, scalar2=None, op0=ALU.mult)), reads=[smb], writes=[smb])
                            P.op("dve", (lambda e, den=den, neg=neg: e.scalar_tensor_tensor(out=den, in0=neg, scalar=1.0, in1=den, op0=ALU.max, op1=ALU.max)), reads=[smb], writes=[smb])
                            P.op("dve", (lambda e, den=den: e.reciprocal(out=den, in_=den)), reads=[smb], writes=[smb])
                            P.op("dve", (lambda e, den=den, cf=cf, EB=EB: e.tensor_tensor(out=cf, in0=den, in1=EB[:, cs], op=ALU.mult)), reads=[smb, B_G], writes=[smb])
                        hh, hhb = HH.next()
                        tt, ttb = TT_.next()
                        bc = lambda ap: ap.unsqueeze(2).to_broadcast([128, HM, 256])
                        P.op("dve", (lambda e: e.tensor_tensor(out=hh[:], in0=rf[:, :, 0:256], in1=bc(sm[:, 1, :]), op=ALU.mult)), reads=[rfb, smb], writes=[hhb])
                        P.op("pool", (lambda e: e.tensor_tensor(out=tt[:], in0=rb[:, :, 0:256], in1=bc(sm[:, 3, :]), op=ALU.mult)), reads=[rbb, smb], writes=[ttb])
                        P.op("dve", (lambda e: e.tensor_tensor(out=hh[:], in0=hh[:], in1=tt[:], op=ALU.add)), reads=[hhb, ttb], writes=[hhb])
                        P.op("dve", (lambda e: e.tensor_reduce(out=sm[:, 4, :], in_=hh[:], axis=AX.X, op=ALU.add)), reads=[hhb, smb], writes=[smb])
                        P.op("pool", (lambda e: e.tensor_tensor(out=tt[:], in0=hh[:], in1=hh[:], op=ALU.mult)), reads=[hhb, ttb], writes=[ttb])
                        P.op("dve", (lambda e: e.tensor_reduce(out=sm[:, 5, :], in_=tt[:], axis=AX.X, op=ALU.add)), reads=[ttb, smb], writes=[smb])
                        P.op("dve", (lambda e: e.tensor_scalar(out=sm[:, 4, :], in0=sm[:, 4, :], scalar1=1.0 / 256, scalar2=None, op0=ALU.mult)), reads=[smb], writes=[smb])
                        P.op("dve", (lambda e: e.tensor_tensor(out=sm[:, 6, :], in0=sm[:, 4, :], in1=sm[:, 4, :], op=ALU.mult)), reads=[smb], writes=[smb])
                        P.op("dve", (lambda e: e.scalar_tensor_tensor(out=sm[:, 5, :], in0=sm[:, 5, :], scalar=1.0 / 256, in1=sm[:, 6, :], op0=ALU.mult, op1=ALU.subtract)), reads=[smb], writes=[smb])
                        P.op("dve", (lambda e: e.tensor_scalar(out=sm[:, 5, :], in0=sm[:, 5, :], scalar1=NORM_EPS, scalar2=None, op0=ALU.add)), reads=[smb], writes=[smb])
                        P.op("act", (lambda e: e.activation(out=sm[:, 5, :], in_=sm[:, 5, :], func=AF.Sqrt)), reads=[smb], writes=[smb])
                        P.op("dve", (lambda e: e.reciprocal(out=sm[:, 5, :], in_=sm[:, 5, :])), reads=[smb], writes=[smb])
                        P.op("dve", (lambda e: e.tensor_tensor(out=hh[:], in0=hh[:], in1=bc(sm[:, 4, :]), op=ALU.subtract)), reads=[hhb, smb], writes=[hhb])
                        P.op("dve", (lambda e: e.tensor_tensor(out=hh[:], in0=hh[:], in1=bc(sm[:, 5, :]), op=ALU.mult)), reads=[hhb, smb], writes=[hhb])
                        hflat = hh[:].rearrange("p h e -> p (h e)")
                        P.op("pool", (lambda e: e.tensor_tensor(out=hflat, in0=hflat, in1=hnw[:], op=ALU.mult)), reads=[hhb, B_hnw], writes=[hhb])
                        yb, ybb = YB.next()
                        P.op("dve", (lambda e: e.tensor_tensor(out=yb[:], in0=hflat, in1=om[:], op=ALU.mult)), reads=[hhb, omb], writes=[ybb])
                        ys, ysb = YS.next()
                        for k4 in range(0, MW // 128, 4):
                            pb, pbb = bankrot.next()
                            pbv = pb[:].bitcast(BF16)

                            def tr(e, pbv=pbv, yb=yb, k4=k4):
                                ins = None
                                for q in range(4):
                                    ins = e.transpose(pbv[:, q * 128:(q + 1) * 128], yb[:, (k4 + q) * 128:(k4 + q + 1) * 128], ident_b[:])
                                return ins
                            P.op("pe", tr, reads=[ybb, B_identb], writes=[pbb])
                            copy_evac(ys[:, k4:k4 + 4, :], pbv[:, 0:512].rearrange("p (q t) -> p q t", q=4), [pbb], [ysb])
                        scr_store("pool", (lambda e, ys=ys, ch=ch: e.dma_start(out=YT[AO:AO + MW, ch * 128:(ch + 1) * 128].rearrange("(k p) t -> p k t", p=128), in_=ys[:])), ysb, "YT")
                    for ch in range(NCH):
                        do_chunk(ch)
                    P.barrier()
            cur[0] = st

        def phaseC():
            with contextlib.ExitStack() as pst:
                cur[0] = pst
                K = make_core(5)
                xT, B_xT, hT, B_hT, fpool, wq = K.xT, K.B_xT, K.hT, K.B_hT, K.fpool, K.wq
                YTt = sb("YTt", [128, KY, 512], BF16)
                B_yt = Buf("YTt")
                o32 = Rot([(sb(f"o32_{i}", [128, 4, 512], F32), Buf(f"o32_{i}")) for i in range(2)])
                X1v = X1T.rearrange("(k p) t -> p k t", p=128)
                YTv = YT.rearrange("(k p) t -> p k t", p=128)
                B_xld = Buf("xld")
                B_out = Buf("outst")

                def plan_tile():
                    for dc in range(DC):
                        wq.add(cWb[dc, :, :], KY * 128)
                    for dc in range(DC):
                        wq.add(cWout[dc, :, :], DC * 128)
                    K.plan_ffn(2)

                def tile(i):
                    cols = slice(i * 512, (i + 1) * 512)
                    for k0 in range(0, DC, 8):
                        nk = min(8, DC - k0)
                        P.dma("sp", (lambda e, k0=k0, nk=nk: e.dma_start(out=xT[:, k0:k0 + nk, :], in_=X1v[:, k0:k0 + nk, cols])),
                              B_xld, reads=[B_scr["X1T"]], writes=B_xT[k0:k0 + nk])
                    P.seal(B_xld, B_xT, "w")
                    for k0 in range(0, KY, 8):
                        nk = min(8, KY - k0)
                        P.dma("sp", (lambda e, k0=k0, nk=nk: e.dma_start(out=YTt[:, k0:k0 + nk, :], in_=YTv[:, k0:k0 + nk, cols])),
                              B_yt, reads=[B_scr["YT"]], writes=[B_yt])
                    for dc in range(DC):
                        wt, wb = wq.get()
                        pA, pAb = bankrot.next()
                        pM, pMb = bankrot.next()

                        def mm(e, wt=wt, pA=pA, pM=pM):
                            ins = None
                            for k in range(KA):
                                ins = e.matmul(pA[:], lhsT=wt[:, k * 128:(k + 1) * 128], rhs=YTt[:, k, :], start=(k == 0), stop=(k == KA - 1))
                            for k in range(KA, KY):
                                ins = e.matmul(pM[:], lhsT=wt[:, k * 128:(k + 1) * 128], rhs=YTt[:, k, :], start=(k == KA), stop=(k == KY - 1))
                            return ins
                        P.op("pe", mm, reads=[wb, B_yt], writes=[pAb, pMb])
                        ga, gab = fpool.next()
                        gm, gmb = fpool.next()
                        P.dma("sp", (lambda e, ga=ga, dc=dc: e.dma_start(out=ga[:], in_=GT[dc * 128:(dc + 1) * 128, cols])), gab, reads=[B_scr["GT"]], writes=[gab])
                        P.dma("sp", (lambda e, gm=gm, dc=dc: e.dma_start(out=gm[:], in_=GT[D + dc * 128:D + (dc + 1) * 128, cols])), gmb, reads=[B_scr["GT"]], writes=[gmb])
                        P.op("dve", (lambda e, ga=ga, pA=pA: e.tensor_tensor(out=ga[:], in0=ga[:], in1=pA[:], op=ALU.mult)), reads=[gab, pAb], writes=[gab])
                        P.op("dve", (lambda e, gm=gm, pM=pM: e.tensor_tensor(out=gm[:], in0=gm[:], in1=pM[:], op=ALU.mult)), reads=[gmb, pMb], writes=[gmb])
                        P.op("pool", (lambda e, ga=ga, gm=gm, dc=dc: e.tensor_tensor(out=hT[:, dc, :], in0=ga[:], in1=gm[:], op=ALU.add)), reads=[gab, gmb], writes=[B_hT[dc]])
                    for dc in range(DC):
                        wt, wb = wq.get()
                        pb, pbb = bankrot.next()

                        def mm(e, wt=wt, pb=pb):
                            ins = None
                            for k in range(DC):
                                ins = e.matmul(pb[:], lhsT=wt[:, k * 128:(k + 1) * 128], rhs=hT[:, k, :], start=(k == 0), stop=(k == DC - 1))
                            return ins
                        P.op("pe", mm, reads=[wb] + B_hT, writes=[pbb])
                        P.op("dve", (lambda e, dc=dc, pb=pb: e.tensor_tensor(out=xT[:, dc, :], in0=xT[:, dc, :], in1=pb[:], op=ALU.add)),
                             reads=[pbb, B_xT[dc]], writes=[B_xT[dc]])
                    K.rmsnorm_T(2)
                    K.ffn(2)
                    K.rms_stats()
                    for dc0 in range(0, DC, 4):
                        nd = min(4, DC - dc0)
                        ot, otb = o32.next()
                        for q in range(nd):
                            dc = dc0 + q
                            P.op("dve", (lambda e, ot=ot, q=q, dc=dc: e.scalar_tensor_tensor(out=ot[:, q, :], in0=xT[:, dc, :], scalar=gains[:, 3, dc:dc + 1],
                                                                                               in1=K.rstd_bc[:], op0=ALU.mult, op1=ALU.mult)),
                                 reads=[B_xT[dc], B_gains, K.B_rstd], writes=[otb])
                        for tcn in range(4):
                            pb, pbb = bankrot.next()

                            def tr(e, pb=pb, ot=ot, tcn=tcn, nd=nd):
                                ins = None
                                for q in range(nd):
                                    ins = e.transpose(pb[:, q * 128:(q + 1) * 128], ot[:, q, tcn * 128:(tcn + 1) * 128], ident_f[:])
                                return ins
                            P.op("pe", tr, reads=[otb, B_ident], writes=[pbb])
                            os_, osb = fpool.next()
                            copy_evac(os_[:, 0:nd * 128], pb[:, 0:nd * 128], [pbb], [osb])
                            r0 = i * 512 + tcn * 128
                            scr_store("pool", (lambda e, os_=os_, r0=r0, dc0=dc0, nd=nd: e.dma_start(out=out[r0:r0 + 128, dc0 * 128:(dc0 + nd) * 128], in_=os_[:, 0:nd * 128])),
                                      osb, "out")

                for i in range(NT):
                    plan_tile()
                for i in range(NT):
                    tile(i)
                P.barrier()
            cur[0] = st

        if "nopre" not in dbg_names:
            prepass()
        if "pa0" not in dbg_names:
            phaseA()
        if upto >= 2:
            conv_pass()
        if upto >= 3:
            attention()
        if upto >= 4:
            mlstm()
        if upto >= 6:
            phaseC()
        P.finish(list(B_scr.values()))
        with nc.Block() as block:
            P.replay(block)
    return nc, dbg


_CACHE = {}


def kernel(**inputs):
    cfg = Cfg()
    if "nc" not in _CACHE:
        _CACHE["nc"] = build(cfg)[0]
    nc = _CACHE["nc"]
    x = np.asarray(inputs["x"], dtype=np.float32)
    B = x.shape[0]
    shared = {}
    for k, v in inputs.items():
        if k == "x":
            continue
        a = np.asarray(v, dtype=np.float32)
        if a.ndim == 3:
            a = a[0]
        elif a.ndim == 2:
            a = a.reshape(1, -1)
        else:
            a = a.reshape(1, -1)
        shared[k] = np.ascontiguousarray(a)
    in_maps = []
    for b in range(B):
        m = dict(shared)
        m["x"] = np.ascontiguousarray(x[b])
        in_maps.append(m)
    res = run_bass_kernel_spmd(nc, in_maps, core_ids=list(range(B)))
    return np.stack([np.asarray(res.results[b]["out"], dtype=np.float32) for b in range(B)], axis=0)
```
